# Optimizing a Trainium2 kernel written in Bass

```python
import math
import jax, jax.numpy as jnp
from jax import lax
import numpy as np

D_MODEL = 1024
BATCH = 32
SEQ = 2048
DEPTH = 1
DEC_BATCH = 8
DEC_SEQ = 32
PAST_LEN = 2048

CHUNK = 64
Q_BLOCK = 128
SB_HEADS = 8
SB_HEAD_DIM = 64
SB_WIDTH = SB_HEADS * SB_HEAD_DIM
SSM_HEADS = 8
SSM_HEAD_DIM = 64
SSM_WIDTH = SSM_HEADS * SSM_HEAD_DIM
SSM_GROUPS = 2
SSM_HEADS_PER_GROUP = SSM_HEADS // SSM_GROUPS
D_STATE = 128
CONV_WIDTH = 4
CONV_DIM = SSM_WIDTH + 2 * SSM_GROUPS * D_STATE
MIX_WIDTH = SB_WIDTH + SSM_WIDTH
D_FF = -(-8 * D_MODEL // (3 * 256)) * 256
N_MOD = 6
SPLITS = [SB_WIDTH, 2 * SB_WIDTH, 3 * SB_WIDTH, 3 * SB_WIDTH + SSM_WIDTH,
          3 * SB_WIDTH + SSM_WIDTH + CONV_DIM]
IN_DIM = 3 * SB_WIDTH + SSM_WIDTH + CONV_DIM + SSM_HEADS
EPS = 1e-6

kernel_name = "hybrid_stickbreak_ssd_adaln_stream_step"


def rms_norm(x, g):
    xf = x.astype(jnp.float32)
    y = xf * lax.rsqrt(jnp.mean(xf * xf, axis=-1, keepdims=True) + EPS)
    return (y * g.astype(jnp.float32)).astype(x.dtype)


def stick_breaking(q, k, v, q_start):
    tq, tk = q.shape[1], k.shape[1]
    z = jnp.einsum("bqhd,bkhd->bhqk", q, k).astype(jnp.float32) * (SB_HEAD_DIM ** -0.5)
    q_pos = q_start + jnp.arange(tq)
    k_pos = jnp.arange(tk)
    mask = k_pos[None, :] < q_pos[:, None]
    log_beta = jax.nn.log_sigmoid(z)
    log_1mb = jnp.where(mask, log_beta - z, 0.0)
    after = lax.cumsum(log_1mb, axis=3, reverse=True) - log_1mb
    w = jnp.where(mask, jnp.exp(log_beta + after), 0.0)
    return jnp.einsum("bhqk,bkhd->bqhd", w.astype(v.dtype), v)


def ssd_scan(x, dt, a, b_in, c_in, init_state, chunk):
    bsz, l = x.shape[0], x.shape[1]
    nc = l // chunk
    xc = x.reshape(bsz, nc, chunk, SSM_GROUPS, SSM_HEADS_PER_GROUP, SSM_HEAD_DIM)
    dtc = dt.reshape(bsz, nc, chunk, SSM_GROUPS, SSM_HEADS_PER_GROUP)
    bc = b_in.reshape(bsz, nc, chunk, SSM_GROUPS, D_STATE)
    cc = c_in.reshape(bsz, nc, chunk, SSM_GROUPS, D_STATE)
    acum = jnp.cumsum(dtc * a, axis=2)
    causal = jnp.tril(jnp.ones((chunk, chunk), dtype=bool))[:, :, None, None]
    seg = acum[:, :, :, None] - acum[:, :, None, :]
    decay = jnp.exp(jnp.where(causal, seg, -jnp.inf))
    cb = jnp.einsum("bctgn,bcsgn->bctsg", cc, bc).astype(jnp.float32)
    mix = cb[..., None] * decay * dtc[:, :, None]
    y_diag = jnp.einsum("bctsgj,bcsgjp->bctgjp", mix, xc)
    xw = xc * (jnp.exp(acum[:, :, -1:] - acum) * dtc)[..., None]
    chunk_states = jnp.einsum("bctgn,bctgjp->bcgjpn", bc, xw)
    chunk_decay = jnp.exp(acum[:, :, -1])

    def step(state, inp):
        st, dec = inp
        return state * dec[..., None, None] + st, state

    final, prev = lax.scan(step, init_state,
                           (jnp.moveaxis(chunk_states, 1, 0), jnp.moveaxis(chunk_decay, 1, 0)))
    prev = jnp.moveaxis(prev, 0, 1)
    y_off = jnp.einsum("bctgn,bcgjpn->bctgjp", cc, prev) * jnp.exp(acum)[..., None]
    y = (y_diag + y_off).reshape(bsz, l, SSM_GROUPS, SSM_HEADS_PER_GROUP, SSM_HEAD_DIM)
    return y, final


def ssd_mixer(z, xbc_raw, dt_raw, conv_past, ssm_past, conv_w, conv_b, dt_bias, a_log, d_skip, g_ssm_out):
    bsz, l = xbc_raw.shape[0], xbc_raw.shape[1]
    full = jnp.concatenate([conv_past.astype(xbc_raw.dtype), xbc_raw], axis=1)
    conv = conv_b
    for i in range(CONV_WIDTH):
        conv = conv + full[:, i:i + l] * conv_w[i]
    new_conv = full[:, l:]
    xbc = jax.nn.silu(conv)
    xs, bs, cs = jnp.split(xbc, [SSM_WIDTH, SSM_WIDTH + SSM_GROUPS * D_STATE], axis=-1)
    x = xs.reshape(bsz, l, SSM_GROUPS, SSM_HEADS_PER_GROUP, SSM_HEAD_DIM)
    b_in = bs.reshape(bsz, l, SSM_GROUPS, D_STATE)
    c_in = cs.reshape(bsz, l, SSM_GROUPS, D_STATE)
    dt = jax.nn.softplus(dt_raw.astype(jnp.float32) + dt_bias.astype(jnp.float32))
    dt = dt.reshape(bsz, l, SSM_GROUPS, SSM_HEADS_PER_GROUP)
    a = -jnp.exp(a_log.astype(jnp.float32)).reshape(SSM_GROUPS, SSM_HEADS_PER_GROUP)
    init = ssm_past.astype(jnp.float32).reshape(bsz, SSM_GROUPS, SSM_HEADS_PER_GROUP, SSM_HEAD_DIM, D_STATE)
    chunk = CHUNK if l % CHUNK == 0 else l
    y, final = ssd_scan(x, dt, a, b_in, c_in, init, chunk)
    y = y + d_skip.reshape(SSM_GROUPS, SSM_HEADS_PER_GROUP)[..., None] * x
    y = rms_norm(y.reshape(bsz, l, SSM_WIDTH) * jax.nn.silu(z), g_ssm_out)
    return y, new_conv, final.reshape(bsz, SSM_HEADS, SSM_HEAD_DIM, D_STATE)


def trunk_layer(x, c, k_past, v_past, conv_past, ssm_past,
                w_ada, b_ada, g_norm1, w_in, g_q, g_k, g_attn_out, conv_w, conv_b,
                dt_bias, a_log, d_skip, g_ssm_out, w_out, g_norm2, w_gate, w_up, w_down):
    bsz, l, _ = x.shape
    mod = (jax.nn.silu(c) @ w_ada + b_ada)[:, None, :]
    shift1, scale1, gate1, shift2, scale2, gate2 = jnp.split(mod, N_MOD, axis=-1)
    h = rms_norm(x, g_norm1) * (1 + scale1) + shift1
    q, k, v, z, xbc_raw, dt_raw = jnp.split(h @ w_in, SPLITS, axis=-1)
    q = rms_norm(q.reshape(bsz, l, SB_HEADS, SB_HEAD_DIM), g_q)
    k = rms_norm(k.reshape(bsz, l, SB_HEADS, SB_HEAD_DIM), g_k)
    v = v.reshape(bsz, l, SB_HEADS, SB_HEAD_DIM)
    if k_past is None:
        attn = jnp.concatenate(
            [stick_breaking(q[:, s:s + Q_BLOCK], k[:, :s + Q_BLOCK], v[:, :s + Q_BLOCK], s)
             for s in range(0, l, Q_BLOCK)], axis=1)
        conv_past = jnp.zeros((bsz, CONV_WIDTH - 1, CONV_DIM), xbc_raw.dtype)
        ssm_past = jnp.zeros((bsz, SSM_HEADS, SSM_HEAD_DIM, D_STATE), jnp.float32)
    else:
        k_all = jnp.concatenate([k_past, k], axis=1)
        v_all = jnp.concatenate([v_past, v], axis=1)
        attn = stick_breaking(q, k_all, v_all, k_past.shape[1])
    attn = rms_norm(attn.reshape(bsz, l, SB_WIDTH), g_attn_out)
    ssm, new_conv, new_ssm = ssd_mixer(z, xbc_raw, dt_raw, conv_past, ssm_past, conv_w, conv_b,
                                       dt_bias, a_log, d_skip, g_ssm_out)
    mixed = jnp.concatenate([attn.astype(ssm.dtype), ssm], axis=-1) @ w_out
    x = x + gate1 * mixed
    h2 = rms_norm(x, g_norm2) * (1 + scale2) + shift2
    ff = (jax.nn.silu(h2 @ w_gate) * (h2 @ w_up)) @ w_down
    x = x + gate2 * ff
    return x, k, v, new_conv, new_ssm


def setup_inputs(seed: int = 0) -> dict:
    key = jax.random.key(seed)
    ks = jax.random.split(key, 32)
    f32 = jnp.float32

    def nrm(k, shape, scale):
        return jax.random.normal(k, shape, f32) * scale

    dt0 = jnp.exp(jax.random.uniform(ks[16], (DEPTH, SSM_HEADS), f32, math.log(1e-3), math.log(1e-1)))
    return {
        "x_prompt": nrm(ks[0], (BATCH, SEQ, D_MODEL), 1.0),
        "x_sample": nrm(ks[1], (DEC_BATCH, DEC_SEQ, D_MODEL), 1.0),
        "cache_k": nrm(ks[2], (DEPTH, DEC_BATCH, PAST_LEN, SB_HEADS, SB_HEAD_DIM), 1.0),
        "cache_v": nrm(ks[3], (DEPTH, DEC_BATCH, PAST_LEN, SB_HEADS, SB_HEAD_DIM), 1.0),
        "state_conv": nrm(ks[4], (DEPTH, DEC_BATCH, CONV_WIDTH - 1, CONV_DIM), 1.0),
        "state_ssm": nrm(ks[5], (DEPTH, DEC_BATCH, SSM_HEADS, SSM_HEAD_DIM, D_STATE), 0.5),
        "c_prompt": nrm(ks[6], (BATCH, D_MODEL), 1.0),
        "c_sample": nrm(ks[7], (DEC_BATCH, D_MODEL), 1.0),
        "w_ada": nrm(ks[8], (DEPTH, D_MODEL, N_MOD * D_MODEL), D_MODEL ** -0.5),
        "b_ada": nrm(ks[9], (DEPTH, N_MOD * D_MODEL), 0.01),
        "g_norm1": 1.0 + nrm(ks[10], (DEPTH, D_MODEL), 0.02),
        "w_in": nrm(ks[11], (DEPTH, D_MODEL, IN_DIM), D_MODEL ** -0.5),
        "g_q": 1.0 + nrm(ks[12], (DEPTH, SB_HEAD_DIM), 0.02),
        "g_k": 1.0 + nrm(ks[13], (DEPTH, SB_HEAD_DIM), 0.02),
        "g_attn_out": 1.0 + nrm(ks[14], (DEPTH, SB_WIDTH), 0.02),
        "conv_w": nrm(ks[15], (DEPTH, CONV_WIDTH, CONV_DIM), CONV_WIDTH ** -0.5),
        "conv_b": nrm(ks[17], (DEPTH, CONV_DIM), 0.01),
        "dt_bias": dt0 + jnp.log(-jnp.expm1(-dt0)),
        "a_log": jnp.log(jax.random.uniform(ks[18], (DEPTH, SSM_HEADS), f32, 1.0, 16.0)),
        "d_skip": 1.0 + nrm(ks[19], (DEPTH, SSM_HEADS), 0.02),
        "g_ssm_out": 1.0 + nrm(ks[20], (DEPTH, SSM_WIDTH), 0.02),
        "w_out": nrm(ks[21], (DEPTH, MIX_WIDTH, D_MODEL), MIX_WIDTH ** -0.5),
        "g_norm2": 1.0 + nrm(ks[22], (DEPTH, D_MODEL), 0.02),
        "w_gate": nrm(ks[23], (DEPTH, D_MODEL, D_FF), D_MODEL ** -0.5),
        "w_up": nrm(ks[24], (DEPTH, D_MODEL, D_FF), D_MODEL ** -0.5),
        "w_down": nrm(ks[25], (DEPTH, D_FF, D_MODEL), D_FF ** -0.5),
    }


def reference(x_prompt, x_sample, cache_k, cache_v, state_conv, state_ssm, c_prompt, c_sample,
              w_ada, b_ada, g_norm1, w_in, g_q, g_k, g_attn_out, conv_w, conv_b,
              dt_bias, a_log, d_skip, g_ssm_out, w_out, g_norm2, w_gate, w_up, w_down):
    xp, xs = x_prompt, x_sample
    kp_l, vp_l, cp_l, sp_l, ks_l, vs_l, cs_l, ss_l = [], [], [], [], [], [], [], []
    for i in range(DEPTH):
        lw = (w_ada[i], b_ada[i], g_norm1[i], w_in[i], g_q[i], g_k[i], g_attn_out[i], conv_w[i], conv_b[i],
              dt_bias[i], a_log[i], d_skip[i], g_ssm_out[i], w_out[i], g_norm2[i], w_gate[i], w_up[i], w_down[i])
        xp, kp, vp, cp, sp = trunk_layer(xp, c_prompt, None, None, None, None, *lw)
        xs, ks_, vs_, cs_, ss_ = trunk_layer(xs, c_sample, cache_k[i], cache_v[i], state_conv[i], state_ssm[i], *lw)
        kp_l.append(kp); vp_l.append(vp); cp_l.append(cp); sp_l.append(sp)
        ks_l.append(ks_); vs_l.append(vs_); cs_l.append(cs_); ss_l.append(ss_)
    y_prompt = xp.astype(x_prompt.dtype)
    y_sample = xs.astype(x_sample.dtype)
    return (y_prompt, y_sample,
            jnp.stack(kp_l), jnp.stack(vp_l), jnp.stack(cp_l), jnp.stack(sp_l),
            jnp.stack(ks_l), jnp.stack(vs_l), jnp.stack(cs_l), jnp.stack(ss_l))
```

```python
import contextlib
import numpy as np
import concourse.bass as bass
import concourse.mybir as mybir
from concourse.bass_utils import run_bass_kernel_spmd

F32 = mybir.dt.float32
BF16 = mybir.dt.bfloat16
F32R = mybir.dt.float32r
AF = mybir.ActivationFunctionType
ALU = mybir.AluOpType
AX = mybir.AxisListType

SAME_ENGINE_SYNC = True
INTERLEAVE = True
SKIP_BIG = False
SCHEDULE = True
NDUMMY = 6
NCORES = 8
D = 1024
NH = 8
HD = 64
DST = 128
DFF = 2816
NFF = DFF // 128
IN_DIM = 3080
PAST = 2048
DEC_SEQ = 32
EPS = 1e-6
NEG_BIG = -32768.0


class Res:
    __slots__ = ("name", "w", "r", "excl")

    def __init__(self, name, excl=False):
        self.name = name
        self.w = None
        self.r = []
        self.excl = excl


class V:
    __slots__ = ("ap", "res")

    def __init__(self, ap, res):
        self.ap = ap
        self.res = res

    def __getitem__(self, key):
        return V(self.ap[key], self.res)

    def bitcast(self, dt):
        return V(self.ap.bitcast(dt), self.res)

    def bc(self, shape):
        return V(self.ap.to_broadcast(list(shape)), self.res)

    def unsq(self, axis):
        return V(self.ap.unsqueeze(axis), self.res)

    def rr(self, s, **kw):
        return V(self.ap.rearrange(s, **kw), self.res)


class StopBuild(Exception):
    pass


class Op:
    __slots__ = ("eng", "fn", "deps", "sig", "sigval", "dma", "sem", "target", "big", "gi", "cost", "tbl", "fin", "epoch")


class Prog:
    ENGS = ("pe", "act", "dve", "pool", "sp")

    def __init__(self, nc, stack, sb_words, dma_pool=8):
        self.nc = nc
        self.stack = stack
        self.ops = {e: [] for e in self.ENGS}
        self.n = 0
        self.dma_pool = dma_pool
        self.extra = {e: None for e in self.ENGS}
        self.dmas_since_barrier = []
        self.sb_words = sb_words
        big = stack.enter_context(nc.sbuf_tensor("bigsb", [128, sb_words], F32))
        self.big = big
        self.r_words = 2 * 128 + 3 * 512
        self.big_r = stack.enter_context(nc.sbuf_tensor("f32rsb", [128, self.r_words], F32R))
        self.r_off = 0
        self.pools = {}
        self.banks = []
        for i in range(8):
            t = stack.enter_context(nc.psum_tensor(f"bank{i}", [128, 512], F32))
            self.banks.append(V(t[:], [Res(f"bank{i}", excl=True)]))
        self.bank_i = 0

    def pool(self, name, start):
        self.pools[name] = [start, start]

    def alloc(self, pool, shape, dt=F32, name="t"):
        n = 1
        for s in shape[1:]:
            n *= s
        if dt == F32R:
            off = self.r_off
            self.r_off += n
            assert self.r_off <= self.r_words
            ap = self.big_r[:, off:off + n]
            if len(shape) > 2:
                names = " ".join(f"a{i}" for i in range(len(shape) - 1))
                kw = {f"a{i}": shape[i + 1] for i in range(len(shape) - 1)}
                ap = ap.rearrange(f"p ({names}) -> p {names}", **kw)
            return V(ap, [Res(name)])
        words = n if dt in (F32, F32R) else (n + 1) // 2
        p = self.pools[pool]
        off = p[1]
        p[1] += words
        assert p[1] <= self.sb_words, f"SBUF overflow in pool {pool}: {p[1]} > {self.sb_words} ({name})"
        if not hasattr(self, "where"):
            self.where = {}
        self.where[name] = (off, words, tuple(shape), dt)
        ap = self.big[:, off:off + words]
        if dt == F32R:
            ap = ap.bitcast(dt)
        elif dt != F32:
            ap = ap.bitcast(dt)
            if n % 2:
                ap = ap[:, 0:n]
        if len(shape) > 2:
            names = " ".join(f"a{i}" for i in range(len(shape) - 1))
            kw = {f"a{i}": shape[i + 1] for i in range(len(shape) - 1)}
            ap = ap.rearrange(f"p ({names}) -> p {names}", **kw)
        if shape[0] < 128:
            ap = ap[0:shape[0]]
        return V(ap, [Res(name)])

    def bank(self, group="bg"):
        if not hasattr(self, "rots"):
            self.rots = {"bg": list(range(8)), "att": [3, 4]}
            self.rot_i = {"bg": 0, "att": 0}
        rot = self.rots[group]
        b = self.banks[rot[self.rot_i[group] % len(rot)]]
        self.rot_i[group] += 1
        return b

    def dram(self, ap, name):
        return V(ap, [Res(name)])

    def barrier(self):
        self.epoch = getattr(self, "epoch", 0) + 1

    def add(self, eng, fn, reads=(), writes=(), dma=False):
        op = Op()
        op.eng = eng
        op.fn = fn
        op.dma = dma
        op.sig = False
        op.sigval = None
        op.sem = None
        op.target = None
        op.gi = self.n
        op.epoch = getattr(self, "epoch", 0)
        op.tbl = None
        op.fin = None
        n_el = 1
        if writes:
            for d_ in writes[0].ap.shape[1:]:
                n_el *= d_
        if dma:
            op.cost = 2.0 + n_el * 128 * 4 / 150e3
        elif eng == "pe":
            op.cost = 0.04 + n_el * 0.0006
        elif eng == "act":
            op.cost = 0.2 + n_el * 0.0008
        elif eng == "dve":
            op.cost = 0.12 + n_el * 0.00105
        else:
            op.cost = 0.15 + n_el * 0.0009
        op.big = False
        if eng in ("act", "dve") and not dma and writes:
            shp = writes[0].ap.shape
            n_ = 1
            for d_ in shp[1:]:
                n_ *= d_
            op.big = n_ >= 256
        self.n += 1
        deps = []
        rres = []
        wres = []
        for v in reads:
            if v is None:
                continue
            for r in v.res:
                if r.excl:
                    if r not in wres:
                        wres.append(r)
                elif r not in rres:
                    rres.append(r)
        for v in writes:
            for r in v.res:
                if r not in wres:
                    wres.append(r)
        for r in rres:
            if r.w is not None:
                deps.append(r.w)
        for r in wres:
            if r.w is not None:
                deps.append(r.w)
            deps.extend(r.r)
        for r in rres:
            if r not in wres:
                r.r.append(op)
        for r in wres:
            r.w = op
            r.r = []
        seen = set()
        op.deps = []
        for d in deps:
            if d is op or id(d) in seen:
                continue
            seen.add(id(d))
            op.deps.append(d)
        self.ops[eng].append(op)
        if dma:
            self.dmas_since_barrier.append(op)
        return op

    def mm(self, out, lhsT, rhs, start=True, stop=True, skip=False):
        op = self.add("pe", lambda e: e.matmul(out.ap, lhsT.ap, rhs.ap, start=start, stop=stop, skip_group_check=skip),
                      reads=[lhsT, rhs] + ([] if start else [out]), writes=[out])
        dt_ = lhsT.ap.dtype
        if dt_ == F32:
            op.cost *= 4.0
        elif dt_ == F32R:
            op.cost *= 1.4
        return op

    def tr(self, out, in_, ident):
        return self.add("pe", lambda e: e.transpose(out.ap, in_.ap, ident.ap),
                        reads=[in_, ident], writes=[out])

    def act(self, out, in_, func, bias=None, scale=None, accum=None):
        kw = {}
        reads = [in_]
        if bias is not None:
            if isinstance(bias, V):
                kw["bias"] = bias.ap
                reads.append(bias)
            else:
                kw["bias"] = float(bias)
        if scale is not None:
            if isinstance(scale, V):
                kw["scale"] = scale.ap
                reads.append(scale)
            else:
                kw["scale"] = float(scale)
        writes = [out]
        if accum is not None:
            kw["accum_out"] = accum.ap
            writes.append(accum)
        op = self.add("act", lambda e: e.activation(out.ap, in_.ap, func, **kw), reads=reads, writes=writes)
        op.tbl = "silu" if func == AF.Silu else ("explog" if func in (AF.Exp, AF.Ln) else None)
        return op

    def tt(self, out, a, b, op, eng="dve"):
        return self.add(eng, lambda e: e.tensor_tensor(out.ap, a.ap, b.ap, op), reads=[a, b], writes=[out])

    def ts(self, out, a, s1, s2, op0, op1=None, eng="dve"):
        reads = [a]
        s1a = s1.ap if isinstance(s1, V) else s1
        s2a = s2.ap if isinstance(s2, V) else s2
        if isinstance(s1, V):
            reads.append(s1)
        if isinstance(s2, V):
            reads.append(s2)
        if op1 is None:
            return self.add(eng, lambda e: e.tensor_scalar(out.ap, a.ap, s1a, None, op0), reads=reads, writes=[out])
        return self.add(eng, lambda e: e.tensor_scalar(out.ap, a.ap, s1a, s2a, op0, op1), reads=reads, writes=[out])

    def stt(self, out, in0, scalar, in1, op0, op1, eng="dve"):
        reads = [in0, in1]
        sa = scalar.ap if isinstance(scalar, V) else scalar
        if isinstance(scalar, V):
            reads.append(scalar)
        return self.add(eng, lambda e: e.scalar_tensor_tensor(out.ap, in0.ap, sa, in1.ap, op0, op1),
                        reads=reads, writes=[out])

    def copy(self, out, in_, eng="dve"):
        if eng == "act":
            return self.add("act", lambda e: e.copy(out.ap, in_.ap), reads=[in_], writes=[out])
        return self.add(eng, lambda e: e.tensor_copy(out.ap, in_.ap), reads=[in_], writes=[out])

    def memset(self, out, val, eng="dve"):
        return self.add(eng, lambda e: e.memset(out.ap, val), reads=[], writes=[out])

    def reduce(self, out, in_, op=ALU.add, axis=AX.X, eng="dve"):
        return self.add(eng, lambda e: e.tensor_reduce(out.ap, in_.ap, axis, op), reads=[in_], writes=[out])

    def asel(self, out, in_, pattern, cmp, fill, base, cmul):
        return self.add("pool", lambda e: e.affine_select(out.ap, in_.ap, pattern=pattern, compare_op=cmp, fill=fill,
                                                          base=base, channel_multiplier=cmul),
                        reads=[in_], writes=[out])

    def dma(self, out, in_, eng="sp", **kw):
        return self.add(eng, lambda e: e.dma_start(out.ap, in_.ap, **kw), reads=[in_], writes=[out], dma=True)

    def schedule(self, window=96, lat=0.4):
        import heapq
        ptr = {e: 0 for e in self.ENGS}
        done = {e: [False] * len(self.ops[e]) for e in self.ENGS}
        etime = {e: 0.0 for e in self.ENGS}
        cur_tbl = [None]
        new_order = {e: [] for e in self.ENGS}
        remaining = sum(len(v) for v in self.ops.values())
        cur_epoch = 0
        while remaining:
            best = None
            for e in self.ENGS:
                lst = self.ops[e]
                dn = done[e]
                i = ptr[e]
                n = len(lst)
                while i < n and dn[i]:
                    i += 1
                ptr[e] = i
                cnt_ = 0
                while i < n and cnt_ < window:
                    if not dn[i]:
                        op = lst[i]
                        if op.epoch > cur_epoch:
                            break
                        cnt_ += 1
                        ready = 0.0
                        ok = True
                        for d in op.deps:
                            if d.fin is None:
                                ok = False
                                break
                            t_ = d.fin + (lat if d.eng != e or d.dma else 0.0)
                            if t_ > ready:
                                ready = t_
                        if ok:
                            st = ready if ready > etime[e] else etime[e]
                            if e == "act" and op.tbl is not None and cur_tbl[0] is not None and op.tbl != cur_tbl[0]:
                                st += 2.6
                            key = (st + 0.002 * (i - ptr[e]), op.gi)
                            if best is None or key < best[0]:
                                best = (key, e, i, st)
                            if st <= etime[e] + 1e-9 and cnt_ == 1:
                                break
                    i += 1
            if best is None:
                cur_epoch += 1
                tmax = max(etime.values())
                for e in self.ENGS:
                    etime[e] = tmax
                assert cur_epoch <= getattr(self, "epoch", 0), "scheduler deadlock"
                continue
            _, e, i, st = best
            op = self.ops[e][i]
            if e == "act" and op.tbl is not None:
                if cur_tbl[0] is not None and op.tbl != cur_tbl[0]:
                    pass
                cur_tbl[0] = op.tbl
            if op.dma:
                op.fin = st + op.cost
                etime[e] = st + 0.1
            else:
                op.fin = st + op.cost
                etime[e] = op.fin
            done[e][i] = True
            new_order[e].append(op)
            remaining -= 1
        self.ops = new_order
        self.sched_time = max(etime.values())

    def emit(self):
        nc = self.nc
        stack = self.stack
        if SCHEDULE:
            self.schedule()
        nep = getattr(self, "epoch", 0)
        for k in range(nep):
            snap = []
            for e in self.ENGS:
                last = None
                for op in self.ops[e]:
                    if op.epoch <= k and not op.dma:
                        last = op
                if last is not None:
                    snap.append(last)
                snap.extend(op for op in self.ops[e] if op.dma and op.epoch == k)
            for e in self.ENGS:
                for op in self.ops[e]:
                    if op.epoch > k:
                        ids = set(id(d) for d in op.deps)
                        op.deps = op.deps + [d for d in snap if id(d) not in ids and d is not op]
                        break
        for e in self.ENGS:
            for op in self.ops[e]:
                for d in op.deps:
                    if d.dma:
                        continue
                    if d.eng == op.eng and (d.eng == "pe" or not SAME_ENGINE_SYNC or (SKIP_BIG and d.big and op.big)) and not op.dma:
                        continue
                    d.sig = True
        sems = {e: stack.enter_context(nc.semaphore(f"s_{e}")) for e in self.ENGS}
        dsems = {}
        for e in self.ENGS:
            if any(op.dma for op in self.ops[e]):
                dsems[e] = [stack.enter_context(nc.semaphore(f"d_{e}_{i}")) for i in range(self.dma_pool)]
        for e in self.ENGS:
            c = 0
            k = 0
            for op in self.ops[e]:
                if op.dma:
                    op.sem = dsems[e][k % self.dma_pool]
                    op.target = 16 * (k // self.dma_pool + 1)
                    k += 1
                elif op.sig:
                    c += 1
                    op.sigval = c
        handles = {"pe": "tensor", "act": "scalar", "dve": "vector", "pool": "gpsimd", "sp": "sync"}
        block = stack.enter_context(nc.Block())
        self.nwaits = 0

        def make(ename):
            oplist = self.ops[ename]

            def body(eng):
                known = {}
                for op in oplist:
                    waits = {}
                    for d in op.deps:
                        if d.dma:
                            key, val = d.sem, d.target
                        else:
                            if d.eng == ename and (ename == "pe" or not SAME_ENGINE_SYNC or (SKIP_BIG and d.big and op.big)) and not op.dma:
                                continue
                            key, val = sems[d.eng], d.sigval
                        if known.get(id(key), 0) >= val:
                            continue
                        if id(key) not in waits or waits[id(key)][1] < val:
                            waits[id(key)] = (key, val)
                    if op.dma and op.target > 16:
                        key, val = op.sem, op.target - 16
                        if known.get(id(key), 0) < val and (id(key) not in waits or waits[id(key)][1] < val):
                            waits[id(key)] = (key, val)
                    for key, val in waits.values():
                        eng.wait_ge(key, val)
                        known[id(key)] = val
                        self.nwaits += 1
                    ins = op.fn(eng)
                    if op.dma:
                        ins.then_inc(op.sem, 16)
                    elif op.sig:
                        ins.then_inc(sems[ename], 1)
                if ename in dsems:
                    last = {}
                    for op in oplist:
                        if op.dma:
                            last[id(op.sem)] = (op.sem, op.target)
                    for key, val in last.values():
                        if known.get(id(key), 0) < val:
                            eng.wait_ge(key, val)
            return body

        for e in self.ENGS:
            if self.ops[e]:
                getattr(block, handles[e])(make(e))


def build(nseq=4, seqlen=2048, sb_words=51400, do_f=True, do_sample=True, upto=None):
    nc = bass.Bass("TRN2", target_bir_lowering=False)
    NS = nseq + 1
    NTOK = nseq * seqlen
    assert seqlen % 512 == 0

    def din(name, shape):
        return nc.dram_tensor(name, list(shape), F32, kind="ExternalInput").ap()

    def dout(name, shape):
        return nc.dram_tensor(name, list(shape), F32, kind="ExternalOutput").ap()

    xp_d = din("xp", [NTOK, D]); xs_d = din("xs", [DEC_SEQ, D])
    ck_d = din("ck", [PAST, 512]); cv_d = din("cv", [PAST, 512])
    sconv_d = din("sconv", [3, D]); sssm_d = din("sssm", [512, 128])
    cvec_d = din("cvec", [NS, D])
    wada_d = din("w_ada", [D, 6 * D]); bada_d = din("b_ada", [1, 6 * D])
    g1_d = din("g_norm1", [1, D]); win_d = din("w_in", [D, IN_DIM])
    gq_d = din("g_q", [1, 64]); gk_d = din("g_k", [1, 64]); ga_d = din("g_attn_out", [1, 512])
    cw_d = din("conv_w", [4, D]); cb_d = din("conv_b", [1, D])
    dtb_d = din("dt_bias", [1, 8]); alog_d = din("a_log", [1, 8]); dsk_d = din("d_skip", [1, 8])
    gs_d = din("g_ssm_out", [1, 512]); wout_d = din("w_out", [D, D]); g2_d = din("g_norm2", [1, D])
    wg_d = din("w_gate", [D, DFF]); wu_d = din("w_up", [D, DFF]); wd_d = din("w_down", [DFF, D])

    yp_d = dout("yp", [NTOK, D]); ys_d = dout("ys", [DEC_SEQ, D])
    kp_d = dout("kp", [NTOK, 512]); vp_d = dout("vp", [NTOK, 512])
    convp_d = dout("convp", [nseq, 3, D]); ssmp_d = dout("ssmp", [nseq, 512, 128])
    ks_d = dout("ks", [DEC_SEQ, 512]); vs_d = dout("vs", [DEC_SEQ, 512])
    convs_d = dout("convs", [1, 3, D]); ssms_d = dout("ssms", [1, 512, 128])
    x1s_d = nc.dram_tensor("x1s", [NTOK + DEC_SEQ, D], F32, kind="Internal").ap()
    modd_d = nc.dram_tensor("modd", [NS, 6 * D], F32, kind="Internal").ap()

    with contextlib.ExitStack() as st:
        P = Prog(nc, st, sb_words)
        P.pool("C", 0)
        A = P.alloc
        identf = A("C", [128, 128], F32, "identf")
        identb = A("C", [128, 128], BF16, "identb")
        TI = A("C", [128, 128], F32, "TI")
        TL = A("C", [128, 128], F32, "TL")
        onesf = A("C", [128, 128], F32, "onesf")
        TLr = A("C", [128, 128], F32R, "TLr")
        OmTr = A("C", [128, 128], F32R, "OmTr")
        negm = A("C", [128, 128], BF16, "negm")
        one11 = A("C", [1, 1], F32, "one11")
        g1col = A("C", [128, 8], F32, "g1col"); g2col = A("C", [128, 8], F32, "g2col")
        gacol = A("C", [128, 4], F32, "gacol"); gscol = A("C", [128, 4], F32, "gscol")
        cwcol = A("C", [128, 8, 4], F32, "cwcol"); cbcol = A("C", [128, 8], F32, "cbcol")
        gq_bc = A("C", [128, 64], F32, "gq_bc"); gk_bc = A("C", [128, 64], F32, "gk_bc")
        dtb_bc = A("C", [128, 8], F32, "dtb_bc"); a_bc = A("C", [128, 8], F32, "a_bc"); dsk_bc = A("C", [128, 8], F32, "dsk")
        A1 = A("C", [128, NS, 8], F32, "A1"); S1 = A("C", [128, NS, 8], F32, "S1")
        A2 = A("C", [128, NS, 8], F32, "A2"); S2 = A("C", [128, NS, 8], F32, "S2")
        cend = P.pools["C"][1]

        def D_(ap, name):
            return P.dram(ap, name)

        P.memset(identf, 0.0)
        P.asel(identf, identf, [[-1, 128]], ALU.not_equal, 1.0, 0, 1)
        P.copy(identb, identf)
        P.memset(TI, 1.0)
        P.asel(TI, TI, [[1, 128]], ALU.is_ge, 0.0, 0, -1)
        P.memset(TL, 1.0)
        P.asel(TL, TL, [[-1, 128]], ALU.is_ge, 0.0, 0, 1)
        P.memset(onesf, 1.0)
        P.copy(TLr, TL)
        P.pool("P0", cend + 8 * IN_DIM // 2 + 8 * D // 2 + 2 * IN_DIM)
        omt = A("P0", [128, 128], F32, "omt")
        P.memset(omt, 1.0)
        P.asel(omt, omt, [[1, 128]], ALU.is_gt, 0.0, 0, -1)
        P.copy(OmTr, omt)
        negf = A("P0", [128, 128], F32, "negf")
        P.memset(negf, 0.0)
        P.asel(negf, negf, [[1, 128]], ALU.is_ge, NEG_BIG, 0, -1)
        P.copy(negm, negf)
        P.memset(one11, 1.0)
        P.dma(g1col, D_(g1_d.rearrange("o (kc p) -> p (o kc)", p=128), "g1d"), allow_slow_non_contiguous=True)
        P.dma(g2col, D_(g2_d.rearrange("o (kc p) -> p (o kc)", p=128), "g2d"), allow_slow_non_contiguous=True)
        P.dma(gacol, D_(ga_d.rearrange("o (kc p) -> p (o kc)", p=128), "gad"), allow_slow_non_contiguous=True)
        P.dma(gscol, D_(gs_d.rearrange("o (kc p) -> p (o kc)", p=128), "gsd"), allow_slow_non_contiguous=True)
        for c in range(8):
            P.dma(cwcol[:, c, :], D_(cw_d[:, c * 128:(c + 1) * 128].rearrange("w p -> p w"), "cwd"),
                  allow_slow_non_contiguous=True)
        P.dma(cbcol, D_(cb_d.rearrange("o (kc p) -> p (o kc)", p=128), "cbd"), allow_slow_non_contiguous=True)
        P.dma(gq_bc, D_(gq_d.to_broadcast([128, 64]), "gqd"))
        P.dma(gk_bc, D_(gk_d.to_broadcast([128, 64]), "gkd"))
        P.dma(dtb_bc, D_(dtb_d.to_broadcast([128, 8]), "dtbd"))
        P.dma(a_bc, D_(alog_d.to_broadcast([128, 8]), "alogd"))
        P.dma(dsk_bc, D_(dsk_d.to_broadcast([128, 8]), "dskd"))
        P.act(a_bc, a_bc, AF.Exp)
        P.ts(a_bc, a_bc, -1.0, None, ALU.mult)

        cv_t = A("P0", [NS, D], F32, "cvec"); sc_t = A("P0", [NS, D], F32, "sc")
        scT = A("P0", [128, 8, NS], F32, "scT")
        modtok = A("P0", [NS, 6 * D], F32, "modtok")
        bada_t = A("P0", [NS, 6 * D], F32, "bada")
        wst = [A("P0", [128, 3072], F32, f"wst{i}") for i in range(2)]
        modcol = A("P0", [128, 32, NS], F32, "modcol")
        modd = D_(modd_d, "modd")
        P.dma(cv_t, D_(cvec_d, "cvecd"))
        P.dma(bada_t, D_(bada_d.to_broadcast([NS, 6 * D]), "badad"))
        P.act(sc_t, cv_t, AF.Silu)
        pb = P.bank()
        for kc in range(8):
            P.tr(pb[:, kc * NS:(kc + 1) * NS], sc_t[:, kc * 128:(kc + 1) * 128], identf[0:NS, 0:NS])
        P.copy(scT, pb[:, 0:8 * NS].rr("p (k s) -> p k s", k=8))
        wada = D_(wada_d, "wada")
        li = 0
        for grp in range(2):
            bks = [P.bank() for _ in range(6)]
            for kc in range(8):
                w = wst[li % 2]; li += 1
                P.dma(w, wada[kc * 128:(kc + 1) * 128, grp * 3072:(grp + 1) * 3072])
                for b6 in range(6):
                    P.mm(bks[b6][0:NS, :], scT[:, kc, :], w[:, b6 * 512:(b6 + 1) * 512], start=(kc == 0), stop=(kc == 7))
            for b6 in range(6):
                c0 = grp * 3072 + b6 * 512
                P.tt(modtok[:, c0:c0 + 512], bks[b6][0:NS, :], bada_t[:, c0:c0 + 512], ALU.add)
        P.dma(modd, modtok, eng="pool")
        pb = P.bank()
        for wi, off in enumerate([0, D, 3 * D, 4 * D]):
            for kc in range(8):
                i = wi * 8 + kc
                P.tr(pb[:, i * NS:(i + 1) * NS], modtok[:, off + kc * 128: off + (kc + 1) * 128], identf[0:NS, 0:NS])
        P.copy(modcol, pb[:, 0:32 * NS].rr("p (k s) -> p k s", k=32))
        for s in range(NS):
            for (Aw, Sw, gcol, wsh, wsc) in ((A1, S1, g1col, 0, 1), (A2, S2, g2col, 2, 3)):
                P.copy(Sw[:, s, :], modcol[:, wsh * 8:(wsh + 1) * 8, s])
                P.ts(Aw[:, s, :], modcol[:, wsc * 8:(wsc + 1) * 8, s], 1.0, None, ALU.add)
                P.tt(Aw[:, s, :], Aw[:, s, :], gcol, ALU.mult)

        if upto == "p0":
            P.dma(D_(ys_d, "ysd0"), cv_t[0:1, :].bc([DEC_SEQ, D]) if False else modtok[0:NS, 0:D], eng="pool") if False else None
            P.emit()
            return nc
        P.pool("M", cend)
        win = A("M", [128, 8, IN_DIM], BF16, "win")
        wout = A("M", [128, 8, D], BF16, "wout")
        stg = [A("M", [128, IN_DIM], F32, f"stg{i}") for i in range(2)]
        mstart_after_stage = None
        li = 0
        wind = D_(win_d, "wind"); woutd = D_(wout_d, "woutd")
        for kc in range(8):
            s_ = stg[li % 2]; li += 1
            P.dma(s_, wind[kc * 128:(kc + 1) * 128, :])
            if kc % 2 == 0:
                P.copy(win[:, kc, :], s_)
            else:
                P.copy(win[:, kc, :], s_, eng="act")
        for fc in range(8):
            s_ = stg[li % 2]; li += 1
            P.dma(s_[:, 0:D], woutd[fc * 128:(fc + 1) * 128, :])
            gcol = gacol[:, fc:fc + 1] if fc < 4 else gscol[:, fc - 4:fc - 3]
            P.act(wout[:, fc, :], s_[:, 0:D], AF.Copy, scale=gcol)
        if upto == "mw":
            P.emit()
            return nc
        P.barrier()
        P.pools["M"][1] -= 2 * IN_DIM

        NKB = PAST // 128 + 1
        kT = A("M", [128, 4, NKB * 128], BF16, "kT")
        vtok = A("M", [128, NKB, 512], BF16, "vtok")
        qT2 = [A("M", [128, 4, 512], BF16, f"qT{i}") for i in range(2)]
        kTb = [V(kT.ap[:, :, b_ * 128:(b_ + 1) * 128], [Res(f"kT{b_}")]) for b_ in range(NKB)]
        vtb = [V(vtok.ap[:, b_, :], [Res(f"vt{b_}")]) for b_ in range(NKB)]
        xt = [A("M", [128, D], F32, f"xt{i}") for i in range(1)]
        xr = xt
        xn = A("M", [128, D], BF16, "xn")
        hT = A("M", [128, 8, 512], BF16, "hT")
        tf = [A("M", [128, 512], F32, f"tf{i}") for i in range(5)]
        tb = [A("M", [128, 512], BF16, f"tb{i}") for i in range(4)]
        xraw = [A("M", [128, 3 + 512], F32, f"xraw{i}") for i in range(2)]
        halo = A("M", [128, 8, 3], F32, "halo")
        xbcT = [A("M", [128, 512], BF16, f"xbcT{i}") for i in range(8)]
        xB = [A("M", [128, 768], BF16, f"xB{i}") for i in range(4)]
        dtt = [A("M", [128, 8], F32, f"dt{i}") for i in range(4)]
        dta = [A("M", [128, 8], F32, f"dta{i}") for i in range(4)]
        sm = [A("M", [128, 8], F32, f"sm{i}") for i in range(24)]
        dec = [A("M", [128, 128], F32, f"dec{i}") for i in range(3)]
        mixT = [A("M", [128, 128], BF16, f"mixT{i}") for i in range(8)]
        Sf = A("M", [128, 512], F32, "Sf"); Sb = A("M", [128, 512], BF16, "Sb")
        mxA = A("M", [128, 4, 512], BF16, "mxA"); mxS2 = [A("M", [128, 4, 512], BF16, f"mxS{i}") for i in range(2)]
        NBUF = 3
        Eb = [A("M", [128, 512], F32, f"E{i}") for i in range(NBUF)]
        SPb = [A("M", [128, 512], F32R, f"SP{i}") for i in range(NBUF)]
        Xb = [A("M", [128, 512], F32, f"X{i}") for i in range(2)]
        Wb = [A("M", [128, 512], BF16, f"W{i}") for i in range(2)]
        oacc = A("M", [128, 4, 512], F32, "oacc")
        g1bc = A("M", [128, D], F32, "g1bc")
        zl = A("M", [128, 128], BF16, "zl"); zr = A("M", [128, 256], BF16, "zr")
        P.memset(zl, 0.0)
        P.memset(zr, 0.0)
        nrm = [A("M", [128, 2], F32, f"nrm{i}") for i in range(12)]
        cnt_n = [0]

        def NRM():
            v = nrm[cnt_n[0] % len(nrm)]
            cnt_n[0] += 1
            return v
        print("SBUF words: C", cend, "M end", P.pools["M"][1], "of", sb_words)

        cnt = {"tf": 0, "tb": 0, "sm": 0, "dec": 0, "xraw": 0, "E": 0}

        def T(kind, lst):
            v = lst[cnt[kind] % len(lst)]
            cnt[kind] += 1
            return v

        xpd = D_(xp_d, "xpd"); xsd = D_(xs_d, "xsd")
        kpd = D_(kp_d, "kpd"); vpd = D_(vp_d, "vpd"); ksd = D_(ks_d, "ksd"); vsd = D_(vs_d, "vsd")
        x1s = D_(x1s_d, "x1s")
        ypd = D_(yp_d, "ypd"); ysd = D_(ys_d, "ysd")

        def rstd_from_ss(out, ssv, n, tmp):
            P.act(tmp, ssv, AF.Ln, bias=EPS, scale=1.0 / n)
            P.act(out, tmp, AF.Exp, scale=-0.5)

        def seq_setup(s, is_sample):
            mt = 0
            nsub, nt = (1, DEC_SEQ) if is_sample else (4, 128)
            NT = nsub * nt
            nmac = 1 if is_sample else seqlen // 512
            xsrc = xsd if is_sample else xpd
            row_base = 0 if is_sample else s * seqlen
            x1row_base = NTOK if is_sample else s * seqlen
            kout = ksd if is_sample else kpd
            vout = vsd if is_sample else vpd
            past_blocks = PAST // 128 if is_sample else 0
            tok0 = mt * 512
            qT = qT2[(s * (seqlen // 512) + mt) % 2]
            mxS = mxS2[(s * (seqlen // 512) + mt) % 2]
            if is_sample:
                for c in range(8):
                    P.dma(halo[:, c, :], D_(sconv_d[:, c * 128:(c + 1) * 128].rearrange("w p -> p w"), "sconvd"),
                          allow_slow_non_contiguous=True)
                ckd = D_(ck_d, "ckd"); cvd = D_(cv_d, "cvd")
                for blk in range(past_blocks):
                    t1 = T("tf", tf); t2 = T("tf", tf)
                    P.dma(t1, ckd[blk * 128:(blk + 1) * 128, :])
                    P.dma(t2, cvd[blk * 128:(blk + 1) * 128, :])
                    kb = T("tb", tb)
                    P.copy(kb, t1, eng="act")
                    P.copy(vtb[blk], t2)
                    pbk = P.bank().bitcast(BF16)
                    for pr in range(4):
                        P.tr(pbk[:, pr * 128:(pr + 1) * 128], kb[:, pr * 128:(pr + 1) * 128], identb)
                    P.copy(kTb[blk], pbk[:, 0:512].rr("p (a t) -> p a t", a=4))
                pbs = P.bank()
                for a4 in range(4):
                    t1 = T("tf", tf)
                    P.dma(t1[:, 0:128], D_(sssm_d[a4 * 128:(a4 + 1) * 128, :], "sssmd"))
                    P.tr(pbs[:, a4 * 128:(a4 + 1) * 128], t1[:, 0:128], identf)
                P.copy(Sf, pbs)
                P.copy(Sb, pbs, eng="act")
            else:
                P.memset(halo, 0.0)
                P.memset(Sf, 0.0)
                P.memset(Sb, 0.0)


        def prep(s, is_sample, mt, kv="all"):
            nsub, nt = (1, DEC_SEQ) if is_sample else (4, 128)
            NT = nsub * nt
            nmac = 1 if is_sample else seqlen // 512
            xsrc = xsd if is_sample else xpd
            row_base = 0 if is_sample else s * seqlen
            x1row_base = NTOK if is_sample else s * seqlen
            kout = ksd if is_sample else kpd
            vout = vsd if is_sample else vpd
            past_blocks = PAST // 128 if is_sample else 0
            tok0 = mt * 512
            qT = qT2[(s * (seqlen // 512) + mt) % 2]
            mxS = mxS2[(s * (seqlen // 512) + mt) % 2]
            if kv != "only":
                for j in range(nsub):
                    xtj = xt[0]
                    r0 = row_base + tok0 + j * nt
                    P.dma(xtj[:nt], xsrc[r0:r0 + nt, :])
                    nr = NRM()
                    P.act(xn[:nt], xtj[:nt], AF.Square, accum=nr[:nt, 0:1])
                    t8 = T("sm", sm)
                    rstd_from_ss(nr[:nt, 1:2], nr[:nt, 0:1], D, t8[:nt, 0:1])
                    P.act(xn[:nt], xtj[:nt], AF.Copy, scale=nr[:nt, 1:2])
                    pbk = P.bank().bitcast(BF16)
                    for kc in range(8):
                        P.tr(pbk[:, kc * 128:kc * 128 + nt], xn[:nt, kc * 128:(kc + 1) * 128], identb[:nt, :nt])
                    for kc in range(8):
                        P.act(hT[:, kc, j * nt:(j + 1) * nt], pbk[:, kc * 128:kc * 128 + nt], AF.Identity,
                              scale=A1[:, s, kc:kc + 1], bias=S1[:, s, kc:kc + 1])
                    yield
            for j in range(nsub):
                r0 = row_base + tok0 + j * nt
                kcol = past_blocks * 128 + tok0 + j * nt
                kblk = past_blocks + (tok0 + j * nt) // 128
                for g in range(3):
                    if (kv == "skip" and g > 0) or (kv == "only" and g == 0):
                        continue
                    pg = P.bank()
                    for kc in range(8):
                        P.mm(pg[:nt, :], hT[:, kc, j * nt:(j + 1) * nt], win[:, kc, g * 512:(g + 1) * 512],
                             start=(kc == 0), stop=(kc == 7))
                    yield
                    if g < 2:
                        sq = T("tf", tf)
                        P.act(sq[:nt], pg[:nt], AF.Square)
                        s8 = T("sm", sm); t8 = T("sm", sm); r8 = T("sm", sm)
                        P.reduce(s8[:nt], sq[:nt].rr("p (h d) -> p h d", h=8))
                        rstd_from_ss(r8[:nt], s8[:nt], HD, t8[:nt])
                        yield
                        qn = T("tf", tf)
                        P.tt(qn[:nt].rr("p (h d) -> p h d", h=8), pg[:nt].rr("p (h d) -> p h d", h=8),
                             r8[:nt].unsq(2).bc([nt, 8, 64]), ALU.mult)
                        gbc = gq_bc if g == 0 else gk_bc
                        kb = T("tb", tb)
                        if g == 1:
                            kf = T("tf", tf)
                            P.tt(kf[:nt].rr("p (h d) -> p h d", h=8), qn[:nt].rr("p (h d) -> p h d", h=8),
                                 gbc[:nt].unsq(1).bc([nt, 8, 64]), ALU.mult)
                            P.dma(kout[r0:r0 + nt, :], kf[:nt], eng="pool")
                            P.copy(kb[:nt], kf[:nt], eng="act")
                        else:
                            P.tt(kb[:nt].rr("p (h d) -> p h d", h=8), qn[:nt].rr("p (h d) -> p h d", h=8),
                                 gbc[:nt].unsq(1).bc([nt, 8, 64]), ALU.mult)
                        pbk = P.bank().bitcast(BF16)
                        for pr in range(4):
                            P.tr(pbk[:, pr * 128:pr * 128 + nt], kb[:nt, pr * 128:(pr + 1) * 128], identb[:nt, :nt])
                        src = pbk[:, 0:512].rr("p (a t) -> p a t", a=4)[:, :, 0:nt]
                        if g == 0:
                            P.copy(qT[:, :, j * nt:(j + 1) * nt], src)
                        else:
                            P.copy(kTb[kblk][:, :, 0:nt], src)
                        yield
                    else:
                        vf = T("tf", tf)
                        P.copy(vf[:nt], pg[:nt], eng="act")
                        P.dma(vout[r0:r0 + nt, :], vf[:nt], eng="pool")
                        P.copy(vtb[kblk][:nt, :], pg[:nt])
                        yield
            if kv == "only":
                return
            for c in range(8):
                pc = P.bank()
                for kc in range(8):
                    P.mm(pc[:, :NT], win[:, kc, 2048 + c * 128:2048 + (c + 1) * 128], hT[:, kc, 0:NT],
                         start=(kc == 0), stop=(kc == 7))
                yield
                xw_ = T("xraw", xraw)
                P.copy(xw_[:, 0:3], halo[:, c, :])
                P.copy(xw_[:, 3:3 + NT], pc[:, :NT], eng="act")
                P.copy(halo[:, c, :], xw_[:, NT:NT + 3])
                ca = T("tf", tf)
                P.act(ca[:, :NT], xw_[:, 3:3 + NT], AF.Identity, scale=cwcol[:, c, 3:4], bias=cbcol[:, c:c + 1])
                for i in (2, 1, 0):
                    P.stt(ca[:, :NT], xw_[:, i:i + NT], cwcol[:, c, i:i + 1], ca[:, :NT], ALU.mult, ALU.add)
                P.act(xbcT[c][:, :NT], ca[:, :NT], AF.Silu)
                yield
            if mt == nmac - 1:
                cdst = convs_d[0] if is_sample else convp_d[s]
                cres = [Res("convout")]
                for c in range(8):
                    P.dma(V(cdst[:, c * 128:(c + 1) * 128].rearrange("w p -> p w"), cres), halo[:, c, :], eng="pool",
                          allow_slow_non_contiguous=True)
            for j in range(nsub):
                pbk = P.bank().bitcast(BF16)
                for c in range(6):
                    P.tr(pbk[:nt, c * 128:(c + 1) * 128], xbcT[c][:, j * nt:(j + 1) * nt], identb)
                P.copy(xB[j][:nt, :], pbk[:nt, 0:768])
                yield
            for j in range(nsub):
                sl = slice(j * nt, (j + 1) * nt)
                pz = P.bank()
                for kc in range(8):
                    P.mm(pz[:nt, :], hT[:, kc, sl], win[:, kc, 1536:2048], start=(kc == 0), stop=(kc == 7))
                sz = T("tf", tf)
                P.act(sz[:nt], pz[:nt], AF.Silu)
                yield
                pdt = P.bank()
                for kc in range(8):
                    P.mm(pdt[:nt, 0:8], hT[:, kc, sl], win[:, kc, 3072:3080], start=(kc == 0), stop=(kc == 7))
                dtr = T("sm", sm); ab = T("sm", sm); e1 = T("sm", sm); l1 = T("sm", sm)
                dt_ = dtt[j]; dta_ = dta[j]
                P.tt(dtr[:nt], pdt[:nt, 0:8], dtb_bc[:nt], ALU.add)
                P.act(ab[:nt], dtr[:nt], AF.Abs)
                P.act(e1[:nt], ab[:nt], AF.Exp, scale=-1.0)
                P.act(l1[:nt], e1[:nt], AF.Ln, bias=1.0)
                P.stt(dt_[:nt], dtr[:nt], 0.0, l1[:nt], ALU.max, ALU.add)
                P.tt(dta_[:nt], dt_[:nt], a_bc[:nt], ALU.mult)
                yield
                pa = P.bank()
                P.mm(pa[:nt, 0:8], TI[:nt, :nt], dta_[:nt, :])
                P.mm(pa[:, 8:16], onesf[:nt, :], dta_[:nt, :])
                acol = T("sm", sm); nacol = T("sm", sm); eacol = T("sm", sm); dif = T("sm", sm)
                dw = T("sm", sm); cdec = T("sm", sm)
                P.copy(acol[:nt], pa[:nt, 0:8])
                P.ts(nacol[:nt], pa[:nt, 0:8], -1.0, None, ALU.mult)
                P.act(eacol[:nt], pa[:nt, 0:8], AF.Exp)
                P.tt(dif[:nt], pa[:nt, 8:16], acol[:nt], ALU.subtract)
                P.act(dif[:nt], dif[:nt], AF.Exp)
                P.tt(dw[:nt], dif[:nt], dt_[:nt], ALU.mult)
                P.act(cdec, pa[:, 8:16], AF.Exp)
                yield
                pcb = P.bank()
                for g in range(2):
                    P.mm(pcb[:nt, g * 128:g * 128 + nt], xbcT[4 + g][:, sl], xbcT[6 + g][:, sl])
                for hh in range(2):
                    pd = P.bank()
                    for h4 in range(4):
                        h = hh * 4 + h4
                        o_ = pd[:nt, h4 * 128:h4 * 128 + nt]
                        P.mm(o_, dta_[:nt, h:h + 1].bc([nt, nt]), TI[:nt, :nt], start=True, stop=False)
                        P.mm(o_, identb[:nt, :nt], negm[:nt, :nt], start=False, stop=True)
                        dc = T("dec", dec)
                        P.act(dc[:nt, :nt], o_, AF.Exp, bias=nacol[:nt, h:h + 1])
                        g = h // 4
                        P.stt(mixT[h][:nt, :nt], pcb[:nt, g * 128:g * 128 + nt], dt_[:nt, h:h + 1], dc[:nt, :nt],
                              ALU.mult, ALU.mult)
                        yield
                pyd = P.bank(); pyo = P.bank()
                for h in range(8):
                    P.mm(pyd[:nt, h * 64:(h + 1) * 64], mixT[h][:nt, :nt], xB[j][:nt, h * 64:(h + 1) * 64])
                for g in range(2):
                    P.mm(pyo[:nt, g * 256:(g + 1) * 256], xbcT[6 + g][:, sl], Sb[:, g * 256:(g + 1) * 256])
                yield
                y1 = T("tf", tf); y2 = T("tf", tf)
                h3 = "p (h d) -> p h d"
                P.tt(y1[:nt].rr(h3, h=8), pyo[:nt].rr(h3, h=8), eacol[:nt].unsq(2).bc([nt, 8, 64]), ALU.mult)
                P.tt(y1[:nt], y1[:nt], pyd[:nt], ALU.add)
                P.tt(y2[:nt].rr(h3, h=8), xB[j][:nt, 0:512].rr(h3, h=8), dsk_bc[:nt].unsq(2).bc([nt, 8, 64]), ALU.mult)
                P.tt(y1[:nt], y1[:nt], y2[:nt], ALU.add)
                P.tt(y1[:nt], y1[:nt], sz[:nt], ALU.mult)
                yield
                nr = NRM()
                yn = T("tb", tb)
                P.act(yn[:nt], y1[:nt], AF.Square, accum=nr[:nt, 0:1])
                t8 = T("sm", sm)
                rstd_from_ss(nr[:nt, 1:2], nr[:nt, 0:1], 512, t8[:nt, 0:1])
                P.act(yn[:nt], y1[:nt], AF.Copy, scale=nr[:nt, 1:2])
                pbk = P.bank().bitcast(BF16)
                for fc in range(4):
                    P.tr(pbk[:, fc * 128:fc * 128 + nt], yn[:nt, fc * 128:(fc + 1) * 128], identb[:nt, :nt])
                P.copy(mxS[:, :, sl], pbk[:, 0:512].rr("p (a t) -> p a t", a=4)[:, :, 0:nt])
                yield
                xw = T("tb", tb)
                P.tt(xw[:nt].rr(h3, h=8), xB[j][:nt, 0:512].rr(h3, h=8), dw[:nt].unsq(2).bc([nt, 8, 64]), ALU.mult)
                pcs = P.bank()
                for g in range(2):
                    P.mm(pcs[:, g * 256:(g + 1) * 256], xB[j][:nt, 512 + g * 128:512 + (g + 1) * 128],
                         xw[:nt, g * 256:(g + 1) * 256])
                P.tt(Sf.rr(h3, h=8), Sf.rr(h3, h=8), cdec.unsq(2).bc([128, 8, 64]), ALU.mult)
                P.tt(Sf, Sf, pcs, ALU.add)
                P.copy(Sb, Sf, eng="act")
                yield
            if mt == nmac - 1:
                pbs = P.bank()
                for a4 in range(4):
                    P.tr(pbs[:, a4 * 128:(a4 + 1) * 128], Sf[:, a4 * 128:(a4 + 1) * 128], identf)
                so = T("tf", tf)
                P.copy(so, pbs)
                sdst = ssms_d[0] if is_sample else ssmp_d[s]
                P.dma(D_(sdst.rearrange("(a p) n -> p a n", p=128), "ssmout"), so.rr("p (a n) -> p a n", a=4), eng="pool")

            yield

        def attn(s, is_sample, mt, bg, bg_units):
            nsub, nt = (1, DEC_SEQ) if is_sample else (4, 128)
            NT = nsub * nt
            nmac = 1 if is_sample else seqlen // 512
            xsrc = xsd if is_sample else xpd
            row_base = 0 if is_sample else s * seqlen
            x1row_base = NTOK if is_sample else s * seqlen
            kout = ksd if is_sample else kpd
            vout = vsd if is_sample else vpd
            past_blocks = PAST // 128 if is_sample else 0
            tok0 = mt * 512
            qT = qT2[(s * (seqlen // 512) + mt) % 2]
            mxS = mxS2[(s * (seqlen // 512) + mt) % 2]
            if is_sample:
                blocks = [(PAST, DEC_SEQ, past_blocks, 0)] + [(b * 128, 128, b, None) for b in range(past_blocks - 1, -1, -1)]
            else:
                nb = mt * 4 + 4
                blocks = [(b * 128, 128, b, (b - mt * 4) if b >= mt * 4 else None) for b in range(nb - 1, -1, -1)]
            NQ = NT
            items = [(h, bi, blk) for h in range(8) for bi, blk in enumerate(blocks)]
            nblk = len(blocks)
            state = {}
            P.bank()
            P.rots["bg"] = [5, 6, 7]
            Rbank = [P.banks[0], P.banks[0]]
            Obank = [P.banks[1], P.banks[2]]

            pSd = {}

            def stageQ(it):
                h, bi, (col0, nk, vblk, jj) = it
                q0 = jj * nt if (jj is not None and not is_sample) else 0
                pr, pb0 = h // 2, (h % 2) * 64
                pS = P.bank("att")
                P.mm(pS[:nk, q0:NQ], kTb[vblk][pb0:pb0 + 64, pr, 0:nk], qT[pb0:pb0 + 64, pr, q0:NQ])
                pSd[(h, bi)] = pS

            def stageA1(it):
                h, bi, (col0, nk, vblk, jj) = it
                q0 = jj * nt if (jj is not None and not is_sample) else 0
                pS = pSd.pop((h, bi))
                sl_ = cnt["E"] % NBUF
                s2_ = cnt["E"] % 2
                cnt["E"] += 1
                E = Eb[sl_]; SP = SPb[sl_]; X = Xb[s2_]; W = Wb[s2_]
                state[(h, bi)] = (E, SP, X, W)
                P.act(E[:nk, q0:NQ], pS[:nk, q0:NQ], AF.Exp, scale=0.125)
                if jj is not None:
                    P.asel(E[:nk, q0:NQ], E[:nk, q0:NQ], [[1, NQ - q0]], ALU.is_ge, 0.0, q0 - jj * 128 - 1, -1)

            def stageA2(it):
                h, bi, (col0, nk, vblk, jj) = it
                q0 = jj * nt if (jj is not None and not is_sample) else 0
                E, SP, X, W = state[(h, bi)]
                P.act(SP[:nk, q0:NQ], E[:nk, q0:NQ], AF.Ln, bias=1.0)

            def stageB1a(it):
                h, bi, (col0, nk, vblk, jj) = it
                q0 = jj * nt if (jj is not None and not is_sample) else 0
                E, SP, X, W = state[(h, bi)]
                R = Rbank[h % 2]
                P.mm(R[:, q0:NQ], TLr[:nk, :], SP[:nk, q0:NQ], start=(bi == 0), stop=False, skip=True)

            def stageB1b(it):
                h, bi, (col0, nk, vblk, jj) = it
                q0 = jj * nt if (jj is not None and not is_sample) else 0
                E, SP, X, W = state[(h, bi)]
                R = Rbank[h % 2]
                P.act(X[:nk, q0:NQ], R[:nk, q0:NQ], AF.Exp, scale=-1.0)

            def stageB2(it):
                h, bi, (col0, nk, vblk, jj) = it
                q0 = jj * nt if (jj is not None and not is_sample) else 0
                E, SP, X, W = state[(h, bi)]
                R = Rbank[h % 2]
                if bi != nblk - 1:
                    P.mm(R[:, q0:NQ], OmTr[:nk, :], SP[:nk, q0:NQ], start=False, stop=False, skip=True)
                P.tt(W[:nk, q0:NQ], E[:nk, q0:NQ], X[:nk, q0:NQ], ALU.mult)

            def stageC(it):
                h, bi, (col0, nk, vblk, jj) = it
                q0 = jj * nt if (jj is not None and not is_sample) else 0
                E, SP, X, W = state.pop((h, bi))
                O = Obank[h % 2]
                i0 = q0 // nt
                for i in range(i0, nsub):
                    P.mm(O[:nt, i * 64:(i + 1) * 64], W[:nk, i * nt:(i + 1) * nt], vtb[vblk][:nk, h * 64:(h + 1) * 64],
                         start=(bi == 0 and i == i0), stop=False, skip=True)
                if NDUMMY and not is_sample:
                    for _ in range(NDUMMY):
                        P.mm(O[:, 256:512], zl, zr, start=False, stop=False, skip=True)
                if bi == nblk - 1:
                    P.copy(oacc[:nt, 0:nsub, h * 64:(h + 1) * 64], O[:nt, 0:nsub * 64].rr("p (a d) -> p a d", a=nsub))

            n_it = len(items)
            per_step = -(-bg_units // max(n_it - 4, 1)) if bg is not None else 0
            stageQ(items[0])
            for step in range(n_it + 2):
                if 0 <= step - 1 < n_it:
                    stageB1a(items[step - 1])
                if step + 1 < n_it:
                    stageQ(items[step + 1])
                if step < n_it:
                    stageA1(items[step])
                if 0 <= step - 1 < n_it:
                    stageB1b(items[step - 1])
                if step < n_it:
                    stageA2(items[step])
                if 0 <= step - 2 < n_it:
                    stageC(items[step - 2])
                if 0 <= step - 1 < n_it:
                    stageB2(items[step - 1])
                if bg is not None:
                    for _ in range(per_step):
                        if next(bg, "done") == "done":
                            bg = None
                            break
            if bg is not None:
                for _ in bg:
                    pass
            P.rots["bg"] = list(range(8))

        def tail(s, is_sample, mt):
            nsub, nt = (1, DEC_SEQ) if is_sample else (4, 128)
            NT = nsub * nt
            nmac = 1 if is_sample else seqlen // 512
            xsrc = xsd if is_sample else xpd
            row_base = 0 if is_sample else s * seqlen
            x1row_base = NTOK if is_sample else s * seqlen
            kout = ksd if is_sample else kpd
            vout = vsd if is_sample else vpd
            past_blocks = PAST // 128 if is_sample else 0
            tok0 = mt * 512
            qT = qT2[(s * (seqlen // 512) + mt) % 2]
            mxS = mxS2[(s * (seqlen // 512) + mt) % 2]
            for i in range(nsub):
                nr = NRM()
                on = T("tb", tb)
                P.act(on[:nt], oacc[:nt, i, :], AF.Square, accum=nr[:nt, 0:1])
                t8 = T("sm", sm)
                rstd_from_ss(nr[:nt, 1:2], nr[:nt, 0:1], 512, t8[:nt, 0:1])
                P.act(on[:nt], oacc[:nt, i, :], AF.Copy, scale=nr[:nt, 1:2])
                pbk = P.bank().bitcast(BF16)
                for fc in range(4):
                    P.tr(pbk[:, fc * 128:fc * 128 + nt], on[:nt, fc * 128:(fc + 1) * 128], identb[:nt, :nt])
                P.copy(mxA[:, :, i * nt:(i + 1) * nt], pbk[:, 0:512].rr("p (a t) -> p a t", a=4)[:, :, 0:nt])
            for j in range(nsub):
                r0 = row_base + tok0 + j * nt
                xr_ = xr[0]
                P.dma(xr_[:nt], xsrc[r0:r0 + nt, :])
                for hf in range(2):
                    po = P.bank()
                    for fc in range(8):
                        src = mxA if fc < 4 else mxS
                        P.mm(po[:nt, :], src[:, fc % 4, j * nt:(j + 1) * nt], wout[:, fc, hf * 512:(hf + 1) * 512],
                             start=(fc == 0), stop=(fc == 7))
                    t1 = T("tf", tf)
                    P.tt(t1[:nt], po[:nt, :], g1bc[:nt, hf * 512:(hf + 1) * 512], ALU.mult)
                    P.tt(xr_[:nt, hf * 512:(hf + 1) * 512], xr_[:nt, hf * 512:(hf + 1) * 512], t1[:nt], ALU.add)
                x1r = x1row_base + tok0 + j * nt
                P.dma(x1s[x1r:x1r + nt, :], xr_[:nt], eng="pool")


        def drain(g):
            n = 0
            for _ in g:
                n += 1
            return n

        def load_g1bc(s):
            P.dma(g1bc, V(modd_d[s:s + 1, 2 * D:3 * D].to_broadcast([128, D]), modd.res))

        def next_seq_bg(s1):
            seq_setup(s1, False)
            yield
            yield from prep(s1, False, 0, kv="skip")

        def run_all():
            nmac = seqlen // 512
            load_g1bc(0)
            seq_setup(0, False)
            units = drain(prep(0, False, 0))
            for s in range(nseq):
                for mt in range(nmac):
                    deferred = None
                    if mt + 1 < nmac:
                        bg = prep(s, False, mt + 1)
                    elif s + 1 < nseq:
                        bg = next_seq_bg(s + 1)
                        deferred = s + 1
                    else:
                        bg = None
                    attn(s, False, mt, bg, units)
                    tail(s, False, mt)
                    if deferred is not None:
                        load_g1bc(deferred)
                        drain(prep(deferred, False, 0, kv="only"))
            if do_sample:
                load_g1bc(nseq)
                seq_setup(nseq, True)
                drain(prep(nseq, True, 0))
                attn(nseq, True, 0, None, units)
                tail(nseq, True, 0)

        try:
            run_all()
        except StopBuild:
            P.emit()
            return nc
        P.barrier()

        P.pool("F", cend)
        wg = A("F", [128, 8, DFF], BF16, "wg"); wu = A("F", [128, 8, DFF], BF16, "wu")
        wd = A("F", [128, NFF, D], BF16, "wd")
        xf = [A("F", [128, D], F32, f"xf{i}") for i in range(4)]
        fst = [A("F", [128, DFF], F32, f"fst{i}") for i in range(2)]
        wgd = D_(wg_d, "wgd"); wud = D_(wu_d, "wud"); wdd = D_(wd_d, "wdd")
        li = 0
        for kc in range(8 if do_f else 0):
            for (wdst, wsrc) in ((wg, wgd), (wu, wud)):
                s_ = fst[li % 2]; li += 1
                P.dma(s_, wsrc[kc * 128:(kc + 1) * 128, :])
                if li % 2 == 0:
                    P.copy(wdst[:, kc, :], s_)
                else:
                    P.copy(wdst[:, kc, :], s_, eng="act")
        for f2 in range(NFF // 2 if do_f else 0):
            s_ = fst[li % 2]; li += 1
            P.dma(s_[:, 0:2 * D].rr("p (a n) -> p a n", a=2), wdd[f2 * 256:(f2 + 1) * 256, :].rr("(a p) n -> p a n", p=128))
            if li % 2 == 0:
                P.copy(wd[:, 2 * f2:2 * f2 + 2, :], s_[:, 0:2 * D].rr("p (a n) -> p a n", a=2))
            else:
                P.copy(wd[:, 2 * f2:2 * f2 + 2, :], s_[:, 0:2 * D].rr("p (a n) -> p a n", a=2), eng="act")
        P.barrier()
        P.pools["F"][1] -= 2 * DFF
        aT = A("F", [128, NFF, 512], BF16, "aT")
        h2T = A("F", [128, 8, 512], BF16, "h2T")
        fjunk = A("F", [128, D], BF16, "fjunk"); fxn = A("F", [128, D], BF16, "fxn")
        sg = [A("F", [128, 512], F32, f"sg{i}") for i in range(2)]
        yo = A("F", [128, D], F32, "yo")
        g2bc = A("F", [128, D], F32, "g2bc")
        fss = A("F", [128, 4], F32, "fss"); frs = A("F", [128, 4], F32, "frs"); ftm = A("F", [128, 4], F32, "ftm")
        print("SBUF words: F end", P.pools["F"][1], "of", sb_words)
        fc_ = {"sg": 0}

        def ffn_macro(s, r0, nsub, nt, ydst, yrow):
            NT = nsub * nt
            for j in range(nsub):
                P.dma(xf[j][:nt], x1s[r0 + j * nt:r0 + (j + 1) * nt, :])
                P.act(fjunk[:nt], xf[j][:nt], AF.Square, accum=fss[:nt, 0:1])
                rstd_from_ss(frs[:nt, 0:1], fss[:nt, 0:1], D, ftm[:nt, 0:1])
                P.act(fxn[:nt], xf[j][:nt], AF.Copy, scale=frs[:nt, 0:1])
                pbk = P.bank().bitcast(BF16)
                for kc in range(8):
                    P.tr(pbk[:, kc * 128:kc * 128 + nt], fxn[:nt, kc * 128:(kc + 1) * 128], identb[:nt, :nt])
                for kc in range(8):
                    P.act(h2T[:, kc, j * nt:(j + 1) * nt], pbk[:, kc * 128:kc * 128 + nt], AF.Identity,
                          scale=A2[:, s, kc:kc + 1], bias=S2[:, s, kc:kc + 1])
            for ffc in range(NFF):
                pg = P.bank(); pu = P.bank()
                for kc in range(8):
                    P.mm(pg[:, :NT], wg[:, kc, ffc * 128:(ffc + 1) * 128], h2T[:, kc, 0:NT], start=(kc == 0), stop=(kc == 7))
                for kc in range(8):
                    P.mm(pu[:, :NT], wu[:, kc, ffc * 128:(ffc + 1) * 128], h2T[:, kc, 0:NT], start=(kc == 0), stop=(kc == 7))
                sg_ = sg[fc_["sg"] % 2]; fc_["sg"] += 1
                P.act(sg_[:, :NT], pg[:, :NT], AF.Silu)
                P.tt(aT[:, ffc, 0:NT], sg_[:, :NT], pu[:, :NT], ALU.mult)
            for j in range(nsub):
                for hf in range(2):
                    py = P.bank()
                    for ffc in range(NFF):
                        P.mm(py[:nt, :], aT[:, ffc, j * nt:(j + 1) * nt], wd[:, ffc, hf * 512:(hf + 1) * 512],
                             start=(ffc == 0), stop=(ffc == NFF - 1))
                    hs = slice(hf * 512, (hf + 1) * 512)
                    P.tt(yo[:nt, hs], py[:nt, :], g2bc[:nt, hs], ALU.mult)
                    P.tt(yo[:nt, hs], yo[:nt, hs], xf[j][:nt, hs], ALU.add)
                P.dma(ydst[yrow + j * nt:yrow + (j + 1) * nt, :], yo[:nt], eng="pool")

        for s in range(NS if do_f else 0):
            P.dma(g2bc, V(modd_d[s:s + 1, 5 * D:6 * D].to_broadcast([128, D]), modd.res))
            if s < nseq:
                for mt in range(seqlen // 512):
                    r0 = s * seqlen + mt * 512
                    ffn_macro(s, r0, 4, 128, ypd, r0)
            else:
                ffn_macro(s, NTOK, 1, DEC_SEQ, ysd, 0)

        global LAST_PROG
        LAST_PROG = P
        P.emit()
        print("ops", P.n, "waits", P.nwaits, {e: len(P.ops[e]) for e in P.ENGS})
    return nc


_NC_CACHE = {}


def _get_nc(nseq, seqlen):
    key = (nseq, seqlen)
    if key not in _NC_CACHE:
        _NC_CACHE[key] = build(nseq, seqlen)
    return _NC_CACHE[key]


def make_in_maps(inputs, ncores, nseq, seqlen):
    f = lambda a: np.ascontiguousarray(np.asarray(a, dtype=np.float32))
    xp = f(inputs["x_prompt"]); xs = f(inputs["x_sample"])
    ck = f(inputs["cache_k"])[0]; cv = f(inputs["cache_v"])[0]
    sconv = f(inputs["state_conv"])[0]; sssm = f(inputs["state_ssm"])[0]
    cp = f(inputs["c_prompt"]); cs = f(inputs["c_sample"])
    wnames = ["w_ada", "b_ada", "g_norm1", "w_in", "g_q", "g_k", "g_attn_out", "conv_w", "conv_b", "dt_bias",
              "a_log", "d_skip", "g_ssm_out", "w_out", "g_norm2", "w_gate", "w_up", "w_down"]
    shared = {}
    for n in wnames:
        a = f(inputs[n])[0]
        if a.ndim == 1:
            a = a[None, :]
        shared[n] = np.ascontiguousarray(a)
    maps = []
    for c in range(ncores):
        m = dict(shared)
        m["xp"] = np.ascontiguousarray(xp[c * nseq:(c + 1) * nseq].reshape(nseq * seqlen, D))
        m["xs"] = np.ascontiguousarray(xs[c])
        m["ck"] = np.ascontiguousarray(ck[c].reshape(PAST, 512))
        m["cv"] = np.ascontiguousarray(cv[c].reshape(PAST, 512))
        m["sconv"] = np.ascontiguousarray(sconv[c])
        m["sssm"] = np.ascontiguousarray(sssm[c].reshape(512, 128))
        m["cvec"] = np.ascontiguousarray(np.concatenate([cp[c * nseq:(c + 1) * nseq], cs[c:c + 1]], axis=0))
        maps.append(m)
    return maps


def gather(results, ncores, nseq, seqlen):
    cat = lambda k: [np.asarray(r[k]) for r in results]
    yp = np.concatenate([a.reshape(nseq, seqlen, D) for a in cat("yp")], 0)
    ys = np.stack(cat("ys"), 0)
    kp = np.concatenate([a.reshape(nseq, seqlen, NH, HD) for a in cat("kp")], 0)[None]
    vp = np.concatenate([a.reshape(nseq, seqlen, NH, HD) for a in cat("vp")], 0)[None]
    convp = np.concatenate(cat("convp"), 0)[None]
    ssmp = np.concatenate([a.reshape(nseq, NH, HD, DST) for a in cat("ssmp")], 0)[None]
    ks = np.stack([a.reshape(DEC_SEQ, NH, HD) for a in cat("ks")], 0)[None]
    vs = np.stack([a.reshape(DEC_SEQ, NH, HD) for a in cat("vs")], 0)[None]
    convs = np.concatenate(cat("convs"), 0)[None]
    ssms = np.concatenate([a.reshape(1, NH, HD, DST) for a in cat("ssms")], 0)[None]
    return tuple(np.ascontiguousarray(a, dtype=np.float32) for a in (yp, ys, kp, vp, convp, ssmp, ks, vs, convs, ssms))


def kernel(**inputs):
    nseq, seqlen = 4, 2048
    nc = _get_nc(nseq, seqlen)
    maps = make_in_maps(inputs, NCORES, nseq, seqlen)
    res = run_bass_kernel_spmd(nc, maps, core_ids=list(range(NCORES)))
    return gather(res.results, NCORES, nseq, seqlen)
```

```python
import contextlib
import numpy as np
import concourse.bass as bass
import concourse.mybir as mybir
from concourse.bass_utils import run_bass_kernel_spmd

F32 = mybir.dt.float32
BF16 = mybir.dt.bfloat16
F32R = mybir.dt.float32r
AF = mybir.ActivationFunctionType
ALU = mybir.AluOpType
AX = mybir.AxisListType

SAME_ENGINE_SYNC = True
INTERLEAVE = True
SKIP_BIG = False
SCHEDULE = True
NDUMMY = 6
NCORES = 8
D = 1024
NH = 8
HD = 64
DST = 128
DFF = 2816
NFF = DFF // 128
IN_DIM = 3080
PAST = 2048
DEC_SEQ = 32
EPS = 1e-6
NEG_BIG = -32768.0


class Res:
    __slots__ = ("name", "w", "r", "excl")

    def __init__(self, name, excl=False):
        self.name = name
        self.w = None
        self.r = []
        self.excl = excl


class V:
    __slots__ = ("ap", "res")

    def __init__(self, ap, res):
        self.ap = ap
        self.res = res

    def __getitem__(self, key):
        return V(self.ap[key], self.res)

    def bitcast(self, dt):
        return V(self.ap.bitcast(dt), self.res)

    def bc(self, shape):
        return V(self.ap.to_broadcast(list(shape)), self.res)

    def unsq(self, axis):
        return V(self.ap.unsqueeze(axis), self.res)

    def rr(self, s, **kw):
        return V(self.ap.rearrange(s, **kw), self.res)


class StopBuild(Exception):
    pass


class Op:
    __slots__ = ("eng", "fn", "deps", "sig", "sigval", "dma", "sem", "target", "big", "gi", "cost", "tbl", "fin", "epoch")


class Prog:
    ENGS = ("pe", "act", "dve", "pool", "sp")

    def __init__(self, nc, stack, sb_words, dma_pool=8):
        self.nc = nc
        self.stack = stack
        self.ops = {e: [] for e in self.ENGS}
        self.n = 0
        self.dma_pool = dma_pool
        self.extra = {e: None for e in self.ENGS}
        self.dmas_since_barrier = []
        self.sb_words = sb_words
        big = stack.enter_context(nc.sbuf_tensor("bigsb", [128, sb_words], F32))
        self.big = big
        self.r_words = 2 * 128 + 3 * 512
        self.big_r = stack.enter_context(nc.sbuf_tensor("f32rsb", [128, self.r_words], F32R))
        self.r_off = 0
        self.pools = {}
        self.banks = []
        for i in range(8):
            t = stack.enter_context(nc.psum_tensor(f"bank{i}", [128, 512], F32))
            self.banks.append(V(t[:], [Res(f"bank{i}", excl=True)]))
        self.bank_i = 0

    def pool(self, name, start):
        self.pools[name] = [start, start]

    def alloc(self, pool, shape, dt=F32, name="t"):
        n = 1
        for s in shape[1:]:
            n *= s
        if dt == F32R:
            off = self.r_off
            self.r_off += n
            assert self.r_off <= self.r_words
            ap = self.big_r[:, off:off + n]
            if len(shape) > 2:
                names = " ".join(f"a{i}" for i in range(len(shape) - 1))
                kw = {f"a{i}": shape[i + 1] for i in range(len(shape) - 1)}
                ap = ap.rearrange(f"p ({names}) -> p {names}", **kw)
            return V(ap, [Res(name)])
        words = n if dt in (F32, F32R) else (n + 1) // 2
        p = self.pools[pool]
        off = p[1]
        p[1] += words
        assert p[1] <= self.sb_words, f"SBUF overflow in pool {pool}: {p[1]} > {self.sb_words} ({name})"
        if not hasattr(self, "where"):
            self.where = {}
        self.where[name] = (off, words, tuple(shape), dt)
        ap = self.big[:, off:off + words]
        if dt == F32R:
            ap = ap.bitcast(dt)
        elif dt != F32:
            ap = ap.bitcast(dt)
            if n % 2:
                ap = ap[:, 0:n]
        if len(shape) > 2:
            names = " ".join(f"a{i}" for i in range(len(shape) - 1))
            kw = {f"a{i}": shape[i + 1] for i in range(len(shape) - 1)}
            ap = ap.rearrange(f"p ({names}) -> p {names}", **kw)
        if shape[0] < 128:
            ap = ap[0:shape[0]]
        return V(ap, [Res(name)])

    def bank(self, group="bg"):
        if not hasattr(self, "rots"):
            self.rots = {"bg": list(range(8)), "att": [3, 4]}
            self.rot_i = {"bg": 0, "att": 0}
        rot = self.rots[group]
        b = self.banks[rot[self.rot_i[group] % len(rot)]]
        self.rot_i[group] += 1
        return b

    def dram(self, ap, name):
        return V(ap, [Res(name)])

    def barrier(self):
        self.epoch = getattr(self, "epoch", 0) + 1

    def add(self, eng, fn, reads=(), writes=(), dma=False):
        op = Op()
        op.eng = eng
        op.fn = fn
        op.dma = dma
        op.sig = False
        op.sigval = None
        op.sem = None
        op.target = None
        op.gi = self.n
        op.epoch = getattr(self, "epoch", 0)
        op.tbl = None
        op.fin = None
        n_el = 1
        if writes:
            for d_ in writes[0].ap.shape[1:]:
                n_el *= d_
        if dma:
            op.cost = 2.0 + n_el * 128 * 4 / 150e3
        elif eng == "pe":
            op.cost = 0.04 + n_el * 0.0006
        elif eng == "act":
            op.cost = 0.2 + n_el * 0.0008
        elif eng == "dve":
            op.cost = 0.12 + n_el * 0.00105
        else:
            op.cost = 0.15 + n_el * 0.0009
        op.big = False
        if eng in ("act", "dve") and not dma and writes:
            shp = writes[0].ap.shape
            n_ = 1
            for d_ in shp[1:]:
                n_ *= d_
            op.big = n_ >= 256
        self.n += 1
        deps = []
        rres = []
        wres = []
        for v in reads:
            if v is None:
                continue
            for r in v.res:
                if r.excl:
                    if r not in wres:
                        wres.append(r)
                elif r not in rres:
                    rres.append(r)
        for v in writes:
            for r in v.res:
                if r not in wres:
                    wres.append(r)
        for r in rres:
            if r.w is not None:
                deps.append(r.w)
        for r in wres:
            if r.w is not None:
                deps.append(r.w)
            deps.extend(r.r)
        for r in rres:
            if r not in wres:
                r.r.append(op)
        for r in wres:
            r.w = op
            r.r = []
        seen = set()
        op.deps = []
        for d in deps:
            if d is op or id(d) in seen:
                continue
            seen.add(id(d))
            op.deps.append(d)
        self.ops[eng].append(op)
        if dma:
            self.dmas_since_barrier.append(op)
        return op

    def mm(self, out, lhsT, rhs, start=True, stop=True, skip=False):
        op = self.add("pe", lambda e: e.matmul(out.ap, lhsT.ap, rhs.ap, start=start, stop=stop, skip_group_check=skip),
                      reads=[lhsT, rhs] + ([] if start else [out]), writes=[out])
        dt_ = lhsT.ap.dtype
        if dt_ == F32:
            op.cost *= 4.0
        elif dt_ == F32R:
            op.cost *= 1.4
        return op

    def tr(self, out, in_, ident):
        return self.add("pe", lambda e: e.transpose(out.ap, in_.ap, ident.ap),
                        reads=[in_, ident], writes=[out])

    def act(self, out, in_, func, bias=None, scale=None, accum=None):
        kw = {}
        reads = [in_]
        if bias is not None:
            if isinstance(bias, V):
                kw["bias"] = bias.ap
                reads.append(bias)
            else:
                kw["bias"] = float(bias)
        if scale is not None:
            if isinstance(scale, V):
                kw["scale"] = scale.ap
                reads.append(scale)
            else:
                kw["scale"] = float(scale)
        writes = [out]
        if accum is not None:
            kw["accum_out"] = accum.ap
            writes.append(accum)
        op = self.add("act", lambda e: e.activation(out.ap, in_.ap, func, **kw), reads=reads, writes=writes)
        op.tbl = "silu" if func == AF.Silu else ("explog" if func in (AF.Exp, AF.Ln) else None)
        return op

    def tt(self, out, a, b, op, eng="dve"):
        return self.add(eng, lambda e: e.tensor_tensor(out.ap, a.ap, b.ap, op), reads=[a, b], writes=[out])

    def ts(self, out, a, s1, s2, op0, op1=None, eng="dve"):
        reads = [a]
        s1a = s1.ap if isinstance(s1, V) else s1
        s2a = s2.ap if isinstance(s2, V) else s2
        if isinstance(s1, V):
            reads.append(s1)
        if isinstance(s2, V):
            reads.append(s2)
        if op1 is None:
            return self.add(eng, lambda e: e.tensor_scalar(out.ap, a.ap, s1a, None, op0), reads=reads, writes=[out])
        return self.add(eng, lambda e: e.tensor_scalar(out.ap, a.ap, s1a, s2a, op0, op1), reads=reads, writes=[out])

    def stt(self, out, in0, scalar, in1, op0, op1, eng="dve"):
        reads = [in0, in1]
        sa = scalar.ap if isinstance(scalar, V) else scalar
        if isinstance(scalar, V):
            reads.append(scalar)
        return self.add(eng, lambda e: e.scalar_tensor_tensor(out.ap, in0.ap, sa, in1.ap, op0, op1),
                        reads=reads, writes=[out])

    def copy(self, out, in_, eng="dve"):
        if eng == "act":
            return self.add("act", lambda e: e.copy(out.ap, in_.ap), reads=[in_], writes=[out])
        return self.add(eng, lambda e: e.tensor_copy(out.ap, in_.ap), reads=[in_], writes=[out])

    def memset(self, out, val, eng="dve"):
        return self.add(eng, lambda e: e.memset(out.ap, val), reads=[], writes=[out])

    def reduce(self, out, in_, op=ALU.add, axis=AX.X, eng="dve"):
        return self.add(eng, lambda e: e.tensor_reduce(out.ap, in_.ap, axis, op), reads=[in_], writes=[out])

    def asel(self, out, in_, pattern, cmp, fill, base, cmul):
        return self.add("pool", lambda e: e.affine_select(out.ap, in_.ap, pattern=pattern, compare_op=cmp, fill=fill,
                                                          base=base, channel_multiplier=cmul),
                        reads=[in_], writes=[out])

    def dma(self, out, in_, eng="sp", **kw):
        return self.add(eng, lambda e: e.dma_start(out.ap, in_.ap, **kw), reads=[in_], writes=[out], dma=True)

    def schedule(self, window=96, lat=0.25):
        import heapq
        ptr = {e: 0 for e in self.ENGS}
        done = {e: [False] * len(self.ops[e]) for e in self.ENGS}
        etime = {e: 0.0 for e in self.ENGS}
        cur_tbl = [None]
        new_order = {e: [] for e in self.ENGS}
        remaining = sum(len(v) for v in self.ops.values())
        cur_epoch = 0
        while remaining:
            best = None
            for e in self.ENGS:
                lst = self.ops[e]
                dn = done[e]
                i = ptr[e]
                n = len(lst)
                while i < n and dn[i]:
                    i += 1
                ptr[e] = i
                cnt_ = 0
                while i < n and cnt_ < window:
                    if not dn[i]:
                        op = lst[i]
                        if op.epoch > cur_epoch:
                            break
                        cnt_ += 1
                        ready = 0.0
                        ok = True
                        for d in op.deps:
                            if d.fin is None:
                                ok = False
                                break
                            t_ = d.fin + (lat if d.eng != e or d.dma else 0.0)
                            if t_ > ready:
                                ready = t_
                        if ok:
                            st = ready if ready > etime[e] else etime[e]
                            if e == "act" and op.tbl is not None and cur_tbl[0] is not None and op.tbl != cur_tbl[0]:
                                st += 1.0
                            key = (st + 0.002 * (i - ptr[e]), op.gi)
                            if best is None or key < best[0]:
                                best = (key, e, i, st)
                            if st <= etime[e] + 1e-9 and cnt_ == 1:
                                break
                    i += 1
            if best is None:
                cur_epoch += 1
                tmax = max(etime.values())
                for e in self.ENGS:
                    etime[e] = tmax
                assert cur_epoch <= getattr(self, "epoch", 0), "scheduler deadlock"
                continue
            _, e, i, st = best
            op = self.ops[e][i]
            if e == "act" and op.tbl is not None:
                if cur_tbl[0] is not None and op.tbl != cur_tbl[0]:
                    pass
                cur_tbl[0] = op.tbl
            if op.dma:
                op.fin = st + op.cost
                etime[e] = st + 0.1
            else:
                op.fin = st + op.cost
                etime[e] = op.fin
            done[e][i] = True
            new_order[e].append(op)
            remaining -= 1
        self.ops = new_order
        self.sched_time = max(etime.values())

    def emit(self):
        nc = self.nc
        stack = self.stack
        if SCHEDULE:
            self.schedule()
        nep = getattr(self, "epoch", 0)
        for k in range(nep):
            snap = []
            for e in self.ENGS:
                last = None
                for op in self.ops[e]:
                    if op.epoch <= k and not op.dma:
                        last = op
                if last is not None:
                    snap.append(last)
                snap.extend(op for op in self.ops[e] if op.dma and op.epoch == k)
            for e in self.ENGS:
                for op in self.ops[e]:
                    if op.epoch > k:
                        ids = set(id(d) for d in op.deps)
                        op.deps = op.deps + [d for d in snap if id(d) not in ids and d is not op]
                        break
        for e in self.ENGS:
            for op in self.ops[e]:
                for d in op.deps:
                    if d.dma:
                        continue
                    if d.eng == op.eng and (d.eng == "pe" or not SAME_ENGINE_SYNC or (SKIP_BIG and d.big and op.big)) and not op.dma:
                        continue
                    d.sig = True
        sems = {e: stack.enter_context(nc.semaphore(f"s_{e}")) for e in self.ENGS}
        dsems = {}
        for e in self.ENGS:
            if any(op.dma for op in self.ops[e]):
                dsems[e] = [stack.enter_context(nc.semaphore(f"d_{e}_{i}")) for i in range(self.dma_pool)]
        for e in self.ENGS:
            c = 0
            k = 0
            for op in self.ops[e]:
                if op.dma:
                    op.sem = dsems[e][k % self.dma_pool]
                    op.target = 16 * (k // self.dma_pool + 1)
                    k += 1
                elif op.sig:
                    c += 1
                    op.sigval = c
        handles = {"pe": "tensor", "act": "scalar", "dve": "vector", "pool": "gpsimd", "sp": "sync"}
        block = stack.enter_context(nc.Block())
        self.nwaits = 0

        def make(ename):
            oplist = self.ops[ename]

            def body(eng):
                known = {}
                for op in oplist:
                    waits = {}
                    for d in op.deps:
                        if d.dma:
                            key, val = d.sem, d.target
                        else:
                            if d.eng == ename and (ename == "pe" or not SAME_ENGINE_SYNC or (SKIP_BIG and d.big and op.big)) and not op.dma:
                                continue
                            key, val = sems[d.eng], d.sigval
                        if known.get(id(key), 0) >= val:
                            continue
                        if id(key) not in waits or waits[id(key)][1] < val:
                            waits[id(key)] = (key, val)
                    if op.dma and op.target > 16:
                        key, val = op.sem, op.target - 16
                        if known.get(id(key), 0) < val and (id(key) not in waits or waits[id(key)][1] < val):
                            waits[id(key)] = (key, val)
                    for key, val in waits.values():
                        eng.wait_ge(key, val)
                        known[id(key)] = val
                        self.nwaits += 1
                    ins = op.fn(eng)
                    if op.dma:
                        ins.then_inc(op.sem, 16)
                    elif op.sig:
                        ins.then_inc(sems[ename], 1)
                if ename in dsems:
                    last = {}
                    for op in oplist:
                        if op.dma:
                            last[id(op.sem)] = (op.sem, op.target)
                    for key, val in last.values():
                        if known.get(id(key), 0) < val:
                            eng.wait_ge(key, val)
            return body

        for e in self.ENGS:
            if self.ops[e]:
                getattr(block, handles[e])(make(e))


def build(nseq=4, seqlen=2048, sb_words=51400, do_f=True, do_sample=True, upto=None):
    nc = bass.Bass("TRN2", target_bir_lowering=False)
    NS = nseq + 1
    NTOK = nseq * seqlen
    assert seqlen % 512 == 0

    def din(name, shape):
        return nc.dram_tensor(name, list(shape), F32, kind="ExternalInput").ap()

    def dout(name, shape):
        return nc.dram_tensor(name, list(shape), F32, kind="ExternalOutput").ap()

    xp_d = din("xp", [NTOK, D]); xs_d = din("xs", [DEC_SEQ, D])
    ck_d = din("ck", [PAST, 512]); cv_d = din("cv", [PAST, 512])
    sconv_d = din("sconv", [3, D]); sssm_d = din("sssm", [512, 128])
    cvec_d = din("cvec", [NS, D])
    wada_d = din("w_ada", [D, 6 * D]); bada_d = din("b_ada", [1, 6 * D])
    g1_d = din("g_norm1", [1, D]); win_d = din("w_in", [D, IN_DIM])
    gq_d = din("g_q", [1, 64]); gk_d = din("g_k", [1, 64]); ga_d = din("g_attn_out", [1, 512])
    cw_d = din("conv_w", [4, D]); cb_d = din("conv_b", [1, D])
    dtb_d = din("dt_bias", [1, 8]); alog_d = din("a_log", [1, 8]); dsk_d = din("d_skip", [1, 8])
    gs_d = din("g_ssm_out", [1, 512]); wout_d = din("w_out", [D, D]); g2_d = din("g_norm2", [1, D])
    wg_d = din("w_gate", [D, DFF]); wu_d = din("w_up", [D, DFF]); wd_d = din("w_down", [DFF, D])

    yp_d = dout("yp", [NTOK, D]); ys_d = dout("ys", [DEC_SEQ, D])
    kp_d = dout("kp", [NTOK, 512]); vp_d = dout("vp", [NTOK, 512])
    convp_d = dout("convp", [nseq, 3, D]); ssmp_d = dout("ssmp", [nseq, 512, 128])
    ks_d = dout("ks", [DEC_SEQ, 512]); vs_d = dout("vs", [DEC_SEQ, 512])
    convs_d = dout("convs", [1, 3, D]); ssms_d = dout("ssms", [1, 512, 128])
    x1s_d = nc.dram_tensor("x1s", [NTOK + DEC_SEQ, D], F32, kind="Internal").ap()
    modd_d = nc.dram_tensor("modd", [NS, 6 * D], F32, kind="Internal").ap()

    with contextlib.ExitStack() as st:
        P = Prog(nc, st, sb_words)
        P.pool("C", 0)
        A = P.alloc
        identf = A("C", [128, 128], F32, "identf")
        identb = A("C", [128, 128], BF16, "identb")
        TI = A("C", [128, 128], F32, "TI")
        TL = A("C", [128, 128], F32, "TL")
        onesf = A("C", [128, 128], F32, "onesf")
        TLr = A("C", [128, 128], F32R, "TLr")
        OmTr = A("C", [128, 128], F32R, "OmTr")
        negm = A("C", [128, 128], BF16, "negm")
        one11 = A("C", [1, 1], F32, "one11")
        g1col = A("C", [128, 8], F32, "g1col"); g2col = A("C", [128, 8], F32, "g2col")
        gacol = A("C", [128, 4], F32, "gacol"); gscol = A("C", [128, 4], F32, "gscol")
        cwcol = A("C", [128, 8, 4], F32, "cwcol"); cbcol = A("C", [128, 8], F32, "cbcol")
        gq_bc = A("C", [128, 64], F32, "gq_bc"); gk_bc = A("C", [128, 64], F32, "gk_bc")
        dtb_bc = A("C", [128, 8], F32, "dtb_bc"); a_bc = A("C", [128, 8], F32, "a_bc"); dsk_bc = A("C", [128, 8], F32, "dsk")
        A1 = A("C", [128, NS, 8], F32, "A1"); S1 = A("C", [128, NS, 8], F32, "S1")
        A2 = A("C", [128, NS, 8], F32, "A2"); S2 = A("C", [128, NS, 8], F32, "S2")
        cend = P.pools["C"][1]

        def D_(ap, name):
            return P.dram(ap, name)

        P.memset(identf, 0.0)
        P.asel(identf, identf, [[-1, 128]], ALU.not_equal, 1.0, 0, 1)
        P.copy(identb, identf)
        P.memset(TI, 1.0)
        P.asel(TI, TI, [[1, 128]], ALU.is_ge, 0.0, 0, -1)
        P.memset(TL, 1.0)
        P.asel(TL, TL, [[-1, 128]], ALU.is_ge, 0.0, 0, 1)
        P.memset(onesf, 1.0)
        P.copy(TLr, TL)
        P.pool("P0", cend + 8 * IN_DIM // 2 + 8 * D // 2 + 2 * IN_DIM)
        omt = A("P0", [128, 128], F32, "omt")
        P.memset(omt, 1.0)
        P.asel(omt, omt, [[1, 128]], ALU.is_gt, 0.0, 0, -1)
        P.copy(OmTr, omt)
        negf = A("P0", [128, 128], F32, "negf")
        P.memset(negf, 0.0)
        P.asel(negf, negf, [[1, 128]], ALU.is_ge, NEG_BIG, 0, -1)
        P.copy(negm, negf)
        P.memset(one11, 1.0)
        P.dma(g1col, D_(g1_d.rearrange("o (kc p) -> p (o kc)", p=128), "g1d"), allow_slow_non_contiguous=True)
        P.dma(g2col, D_(g2_d.rearrange("o (kc p) -> p (o kc)", p=128), "g2d"), allow_slow_non_contiguous=True)
        P.dma(gacol, D_(ga_d.rearrange("o (kc p) -> p (o kc)", p=128), "gad"), allow_slow_non_contiguous=True)
        P.dma(gscol, D_(gs_d.rearrange("o (kc p) -> p (o kc)", p=128), "gsd"), allow_slow_non_contiguous=True)
        for c in range(8):
            P.dma(cwcol[:, c, :], D_(cw_d[:, c * 128:(c + 1) * 128].rearrange("w p -> p w"), "cwd"),
                  allow_slow_non_contiguous=True)
        P.dma(cbcol, D_(cb_d.rearrange("o (kc p) -> p (o kc)", p=128), "cbd"), allow_slow_non_contiguous=True)
        P.dma(gq_bc, D_(gq_d.to_broadcast([128, 64]), "gqd"))
        P.dma(gk_bc, D_(gk_d.to_broadcast([128, 64]), "gkd"))
        P.dma(dtb_bc, D_(dtb_d.to_broadcast([128, 8]), "dtbd"))
        P.dma(a_bc, D_(alog_d.to_broadcast([128, 8]), "alogd"))
        P.dma(dsk_bc, D_(dsk_d.to_broadcast([128, 8]), "dskd"))
        P.act(a_bc, a_bc, AF.Exp)
        P.ts(a_bc, a_bc, -1.0, None, ALU.mult)

        cv_t = A("P0", [NS, D], F32, "cvec"); sc_t = A("P0", [NS, D], F32, "sc")
        scT = A("P0", [128, 8, NS], F32, "scT")
        modtok = A("P0", [NS, 6 * D], F32, "modtok")
        bada_t = A("P0", [NS, 6 * D], F32, "bada")
        wst = [A("P0", [128, 3072], F32, f"wst{i}") for i in range(2)]
        modcol = A("P0", [128, 32, NS], F32, "modcol")
        modd = D_(modd_d, "modd")
        P.dma(cv_t, D_(cvec_d, "cvecd"))
        P.dma(bada_t, D_(bada_d.to_broadcast([NS, 6 * D]), "badad"))
        P.act(sc_t, cv_t, AF.Silu)
        pb = P.bank()
        for kc in range(8):
            P.tr(pb[:, kc * NS:(kc + 1) * NS], sc_t[:, kc * 128:(kc + 1) * 128], identf[0:NS, 0:NS])
        P.copy(scT, pb[:, 0:8 * NS].rr("p (k s) -> p k s", k=8))
        wada = D_(wada_d, "wada")
        li = 0
        for grp in range(2):
            bks = [P.bank() for _ in range(6)]
            for kc in range(8):
                w = wst[li % 2]; li += 1
                P.dma(w, wada[kc * 128:(kc + 1) * 128, grp * 3072:(grp + 1) * 3072])
                for b6 in range(6):
                    P.mm(bks[b6][0:NS, :], scT[:, kc, :], w[:, b6 * 512:(b6 + 1) * 512], start=(kc == 0), stop=(kc == 7))
            for b6 in range(6):
                c0 = grp * 3072 + b6 * 512
                P.tt(modtok[:, c0:c0 + 512], bks[b6][0:NS, :], bada_t[:, c0:c0 + 512], ALU.add)
        P.dma(modd, modtok, eng="pool")
        pb = P.bank()
        for wi, off in enumerate([0, D, 3 * D, 4 * D]):
            for kc in range(8):
                i = wi * 8 + kc
                P.tr(pb[:, i * NS:(i + 1) * NS], modtok[:, off + kc * 128: off + (kc + 1) * 128], identf[0:NS, 0:NS])
        P.copy(modcol, pb[:, 0:32 * NS].rr("p (k s) -> p k s", k=32))
        for s in range(NS):
            for (Aw, Sw, gcol, wsh, wsc) in ((A1, S1, g1col, 0, 1), (A2, S2, g2col, 2, 3)):
                P.copy(Sw[:, s, :], modcol[:, wsh * 8:(wsh + 1) * 8, s])
                P.ts(Aw[:, s, :], modcol[:, wsc * 8:(wsc + 1) * 8, s], 1.0, None, ALU.add)
                P.tt(Aw[:, s, :], Aw[:, s, :], gcol, ALU.mult)

        if upto == "p0":
            P.dma(D_(ys_d, "ysd0"), cv_t[0:1, :].bc([DEC_SEQ, D]) if False else modtok[0:NS, 0:D], eng="pool") if False else None
            P.emit()
            return nc
        P.pool("M", cend)
        win = A("M", [128, 8, IN_DIM], BF16, "win")
        wout = A("M", [128, 8, D], BF16, "wout")
        stg = [A("M", [128, IN_DIM], F32, f"stg{i}") for i in range(2)]
        mstart_after_stage = None
        li = 0
        wind = D_(win_d, "wind"); woutd = D_(wout_d, "woutd")
        for kc in range(8):
            s_ = stg[li % 2]; li += 1
            P.dma(s_, wind[kc * 128:(kc + 1) * 128, :])
            if kc % 2 == 0:
                P.copy(win[:, kc, :], s_)
            else:
                P.copy(win[:, kc, :], s_, eng="act")
        for fc in range(8):
            s_ = stg[li % 2]; li += 1
            P.dma(s_[:, 0:D], woutd[fc * 128:(fc + 1) * 128, :])
            gcol = gacol[:, fc:fc + 1] if fc < 4 else gscol[:, fc - 4:fc - 3]
            P.act(wout[:, fc, :], s_[:, 0:D], AF.Copy, scale=gcol)
        if upto == "mw":
            P.emit()
            return nc
        P.barrier()
        P.pools["M"][1] -= 2 * IN_DIM

        NKB = PAST // 128 + 1
        kT = A("M", [128, 4, NKB * 128], BF16, "kT")
        vtok = A("M", [128, NKB, 512], BF16, "vtok")
        qT2 = [A("M", [128, 4, 512], BF16, f"qT{i}") for i in range(2)]
        kTb = [V(kT.ap[:, :, b_ * 128:(b_ + 1) * 128], [Res(f"kT{b_}")]) for b_ in range(NKB)]
        vtb = [V(vtok.ap[:, b_, :], [Res(f"vt{b_}")]) for b_ in range(NKB)]
        xt = [A("M", [128, D], F32, f"xt{i}") for i in range(1)]
        xr = xt
        xn = A("M", [128, D], BF16, "xn")
        hT = A("M", [128, 8, 512], BF16, "hT")
        tf = [A("M", [128, 512], F32, f"tf{i}") for i in range(5)]
        tb = [A("M", [128, 512], BF16, f"tb{i}") for i in range(4)]
        xraw = [A("M", [128, 3 + 512], F32, f"xraw{i}") for i in range(2)]
        halo = A("M", [128, 8, 3], F32, "halo")
        xbcT = [A("M", [128, 512], BF16, f"xbcT{i}") for i in range(8)]
        xB = [A("M", [128, 768], BF16, f"xB{i}") for i in range(4)]
        dtt = [A("M", [128, 8], F32, f"dt{i}") for i in range(4)]
        dta = [A("M", [128, 8], F32, f"dta{i}") for i in range(4)]
        sm = [A("M", [128, 8], F32, f"sm{i}") for i in range(24)]
        dec = [A("M", [128, 128], F32, f"dec{i}") for i in range(3)]
        mixT = [A("M", [128, 128], BF16, f"mixT{i}") for i in range(8)]
        Sf = A("M", [128, 512], F32, "Sf"); Sb = A("M", [128, 512], BF16, "Sb")
        mxA = A("M", [128, 4, 512], BF16, "mxA"); mxS2 = [A("M", [128, 4, 512], BF16, f"mxS{i}") for i in range(2)]
        NBUF = 3
        Eb = [A("M", [128, 512], F32, f"E{i}") for i in range(NBUF)]
        SPb = [A("M", [128, 512], F32R, f"SP{i}") for i in range(NBUF)]
        Xb = [A("M", [128, 512], F32, f"X{i}") for i in range(2)]
        Wb = [A("M", [128, 512], BF16, f"W{i}") for i in range(2)]
        oacc = A("M", [128, 4, 512], F32, "oacc")
        g1bc = A("M", [128, D], F32, "g1bc")
        zl = A("M", [128, 128], BF16, "zl"); zr = A("M", [128, 256], BF16, "zr")
        P.memset(zl, 0.0)
        P.memset(zr, 0.0)
        nrm = [A("M", [128, 2], F32, f"nrm{i}") for i in range(12)]
        cnt_n = [0]

        def NRM():
            v = nrm[cnt_n[0] % len(nrm)]
            cnt_n[0] += 1
            return v
        print("SBUF words: C", cend, "M end", P.pools["M"][1], "of", sb_words)

        cnt = {"tf": 0, "tb": 0, "sm": 0, "dec": 0, "xraw": 0, "E": 0}

        def T(kind, lst):
            v = lst[cnt[kind] % len(lst)]
            cnt[kind] += 1
            return v

        xpd = D_(xp_d, "xpd"); xsd = D_(xs_d, "xsd")
        kpd = D_(kp_d, "kpd"); vpd = D_(vp_d, "vpd"); ksd = D_(ks_d, "ksd"); vsd = D_(vs_d, "vsd")
        x1s = D_(x1s_d, "x1s")
        ypd = D_(yp_d, "ypd"); ysd = D_(ys_d, "ysd")

        def rstd_from_ss(out, ssv, n, tmp):
            P.act(tmp, ssv, AF.Ln, bias=EPS, scale=1.0 / n)
            P.act(out, tmp, AF.Exp, scale=-0.5)

        def seq_setup(s, is_sample):
            mt = 0
            nsub, nt = (1, DEC_SEQ) if is_sample else (4, 128)
            NT = nsub * nt
            nmac = 1 if is_sample else seqlen // 512
            xsrc = xsd if is_sample else xpd
            row_base = 0 if is_sample else s * seqlen
            x1row_base = NTOK if is_sample else s * seqlen
            kout = ksd if is_sample else kpd
            vout = vsd if is_sample else vpd
            past_blocks = PAST // 128 if is_sample else 0
            tok0 = mt * 512
            qT = qT2[(s * (seqlen // 512) + mt) % 2]
            mxS = mxS2[(s * (seqlen // 512) + mt) % 2]
            if is_sample:
                for c in range(8):
                    P.dma(halo[:, c, :], D_(sconv_d[:, c * 128:(c + 1) * 128].rearrange("w p -> p w"), "sconvd"),
                          allow_slow_non_contiguous=True)
                ckd = D_(ck_d, "ckd"); cvd = D_(cv_d, "cvd")
                for blk in range(past_blocks):
                    t1 = T("tf", tf); t2 = T("tf", tf)
                    P.dma(t1, ckd[blk * 128:(blk + 1) * 128, :])
                    P.dma(t2, cvd[blk * 128:(blk + 1) * 128, :])
                    kb = T("tb", tb)
                    P.copy(kb, t1, eng="act")
                    P.copy(vtb[blk], t2)
                    pbk = P.bank().bitcast(BF16)
                    for pr in range(4):
                        P.tr(pbk[:, pr * 128:(pr + 1) * 128], kb[:, pr * 128:(pr + 1) * 128], identb)
                    P.copy(kTb[blk], pbk[:, 0:512].rr("p (a t) -> p a t", a=4))
                pbs = P.bank()
                for a4 in range(4):
                    t1 = T("tf", tf)
                    P.dma(t1[:, 0:128], D_(sssm_d[a4 * 128:(a4 + 1) * 128, :], "sssmd"))
                    P.tr(pbs[:, a4 * 128:(a4 + 1) * 128], t1[:, 0:128], identf)
                P.copy(Sf, pbs)
                P.copy(Sb, pbs, eng="act")
            else:
                P.memset(halo, 0.0)
                P.memset(Sf, 0.0)
                P.memset(Sb, 0.0)


        def prep(s, is_sample, mt, kv="all"):
            nsub, nt = (1, DEC_SEQ) if is_sample else (4, 128)
            NT = nsub * nt
            nmac = 1 if is_sample else seqlen // 512
            xsrc = xsd if is_sample else xpd
            row_base = 0 if is_sample else s * seqlen
            x1row_base = NTOK if is_sample else s * seqlen
            kout = ksd if is_sample else kpd
            vout = vsd if is_sample else vpd
            past_blocks = PAST // 128 if is_sample else 0
            tok0 = mt * 512
            qT = qT2[(s * (seqlen // 512) + mt) % 2]
            mxS = mxS2[(s * (seqlen // 512) + mt) % 2]
            if kv != "only":
                for j in range(nsub):
                    xtj = xt[0]
                    r0 = row_base + tok0 + j * nt
                    P.dma(xtj[:nt], xsrc[r0:r0 + nt, :])
                    nr = NRM()
                    P.act(xn[:nt], xtj[:nt], AF.Square, accum=nr[:nt, 0:1])
                    t8 = T("sm", sm)
                    rstd_from_ss(nr[:nt, 1:2], nr[:nt, 0:1], D, t8[:nt, 0:1])
                    P.act(xn[:nt], xtj[:nt], AF.Copy, scale=nr[:nt, 1:2])
                    pbk = P.bank().bitcast(BF16)
                    for kc in range(8):
                        P.tr(pbk[:, kc * 128:kc * 128 + nt], xn[:nt, kc * 128:(kc + 1) * 128], identb[:nt, :nt])
                    for kc in range(8):
                        P.act(hT[:, kc, j * nt:(j + 1) * nt], pbk[:, kc * 128:kc * 128 + nt], AF.Identity,
                              scale=A1[:, s, kc:kc + 1], bias=S1[:, s, kc:kc + 1])
                    yield
            for j in range(nsub):
                r0 = row_base + tok0 + j * nt
                kcol = past_blocks * 128 + tok0 + j * nt
                kblk = past_blocks + (tok0 + j * nt) // 128
                for g in range(3):
                    if (kv == "skip" and g > 0) or (kv == "only" and g == 0):
                        continue
                    pg = P.bank()
                    for kc in range(8):
                        P.mm(pg[:nt, :], hT[:, kc, j * nt:(j + 1) * nt], win[:, kc, g * 512:(g + 1) * 512],
                             start=(kc == 0), stop=(kc == 7))
                    yield
                    if g < 2:
                        sq = T("tf", tf)
                        P.act(sq[:nt], pg[:nt], AF.Square)
                        s8 = T("sm", sm); t8 = T("sm", sm); r8 = T("sm", sm)
                        P.reduce(s8[:nt], sq[:nt].rr("p (h d) -> p h d", h=8))
                        rstd_from_ss(r8[:nt], s8[:nt], HD, t8[:nt])
                        yield
                        qn = T("tf", tf)
                        P.tt(qn[:nt].rr("p (h d) -> p h d", h=8), pg[:nt].rr("p (h d) -> p h d", h=8),
                             r8[:nt].unsq(2).bc([nt, 8, 64]), ALU.mult)
                        gbc = gq_bc if g == 0 else gk_bc
                        kb = T("tb", tb)
                        if g == 1:
                            kf = T("tf", tf)
                            P.tt(kf[:nt].rr("p (h d) -> p h d", h=8), qn[:nt].rr("p (h d) -> p h d", h=8),
                                 gbc[:nt].unsq(1).bc([nt, 8, 64]), ALU.mult)
                            P.dma(kout[r0:r0 + nt, :], kf[:nt], eng="pool")
                            P.copy(kb[:nt], kf[:nt], eng="act")
                        else:
                            P.tt(kb[:nt].rr("p (h d) -> p h d", h=8), qn[:nt].rr("p (h d) -> p h d", h=8),
                                 gbc[:nt].unsq(1).bc([nt, 8, 64]), ALU.mult)
                        pbk = P.bank().bitcast(BF16)
                        for pr in range(4):
                            P.tr(pbk[:, pr * 128:pr * 128 + nt], kb[:nt, pr * 128:(pr + 1) * 128], identb[:nt, :nt])
                        src = pbk[:, 0:512].rr("p (a t) -> p a t", a=4)[:, :, 0:nt]
                        if g == 0:
                            P.copy(qT[:, :, j * nt:(j + 1) * nt], src)
                        else:
                            P.copy(kTb[kblk][:, :, 0:nt], src)
                        yield
                    else:
                        vf = T("tf", tf)
                        P.copy(vf[:nt], pg[:nt], eng="act")
                        P.dma(vout[r0:r0 + nt, :], vf[:nt], eng="pool")
                        P.copy(vtb[kblk][:nt, :], pg[:nt])
                        yield
            if kv == "only":
                return
            for c in range(8):
                pc = P.bank()
                for kc in range(8):
                    P.mm(pc[:, :NT], win[:, kc, 2048 + c * 128:2048 + (c + 1) * 128], hT[:, kc, 0:NT],
                         start=(kc == 0), stop=(kc == 7))
                yield
                xw_ = T("xraw", xraw)
                P.copy(xw_[:, 0:3], halo[:, c, :])
                P.copy(xw_[:, 3:3 + NT], pc[:, :NT], eng="act")
                P.copy(halo[:, c, :], xw_[:, NT:NT + 3])
                ca = T("tf", tf)
                P.act(ca[:, :NT], xw_[:, 3:3 + NT], AF.Identity, scale=cwcol[:, c, 3:4], bias=cbcol[:, c:c + 1])
                for i in (2, 1, 0):
                    P.stt(ca[:, :NT], xw_[:, i:i + NT], cwcol[:, c, i:i + 1], ca[:, :NT], ALU.mult, ALU.add)
                P.act(xbcT[c][:, :NT], ca[:, :NT], AF.Silu)
                yield
            if mt == nmac - 1:
                cdst = convs_d[0] if is_sample else convp_d[s]
                cres = [Res("convout")]
                for c in range(8):
                    P.dma(V(cdst[:, c * 128:(c + 1) * 128].rearrange("w p -> p w"), cres), halo[:, c, :], eng="pool",
                          allow_slow_non_contiguous=True)
            for j in range(nsub):
                pbk = P.bank().bitcast(BF16)
                for c in range(6):
                    P.tr(pbk[:nt, c * 128:(c + 1) * 128], xbcT[c][:, j * nt:(j + 1) * nt], identb)
                P.copy(xB[j][:nt, :], pbk[:nt, 0:768])
                yield
            for j in range(nsub):
                sl = slice(j * nt, (j + 1) * nt)
                pz = P.bank()
                for kc in range(8):
                    P.mm(pz[:nt, :], hT[:, kc, sl], win[:, kc, 1536:2048], start=(kc == 0), stop=(kc == 7))
                sz = T("tf", tf)
                P.act(sz[:nt], pz[:nt], AF.Silu)
                yield
                pdt = P.bank()
                for kc in range(8):
                    P.mm(pdt[:nt, 0:8], hT[:, kc, sl], win[:, kc, 3072:3080], start=(kc == 0), stop=(kc == 7))
                dtr = T("sm", sm); ab = T("sm", sm); e1 = T("sm", sm); l1 = T("sm", sm)
                dt_ = dtt[j]; dta_ = dta[j]
                P.tt(dtr[:nt], pdt[:nt, 0:8], dtb_bc[:nt], ALU.add)
                P.act(ab[:nt], dtr[:nt], AF.Abs)
                P.act(e1[:nt], ab[:nt], AF.Exp, scale=-1.0)
                P.act(l1[:nt], e1[:nt], AF.Ln, bias=1.0)
                P.stt(dt_[:nt], dtr[:nt], 0.0, l1[:nt], ALU.max, ALU.add)
                P.tt(dta_[:nt], dt_[:nt], a_bc[:nt], ALU.mult)
                yield
                pa = P.bank()
                P.mm(pa[:nt, 0:8], TI[:nt, :nt], dta_[:nt, :])
                P.mm(pa[:, 8:16], onesf[:nt, :], dta_[:nt, :])
                acol = T("sm", sm); nacol = T("sm", sm); eacol = T("sm", sm); dif = T("sm", sm)
                dw = T("sm", sm); cdec = T("sm", sm)
                P.copy(acol[:nt], pa[:nt, 0:8])
                P.ts(nacol[:nt], pa[:nt, 0:8], -1.0, None, ALU.mult)
                P.act(eacol[:nt], pa[:nt, 0:8], AF.Exp)
                P.tt(dif[:nt], pa[:nt, 8:16], acol[:nt], ALU.subtract)
                P.act(dif[:nt], dif[:nt], AF.Exp)
                P.tt(dw[:nt], dif[:nt], dt_[:nt], ALU.mult)
                P.act(cdec, pa[:, 8:16], AF.Exp)
                yield
                pcb = P.bank()
                for g in range(2):
                    P.mm(pcb[:nt, g * 128:g * 128 + nt], xbcT[4 + g][:, sl], xbcT[6 + g][:, sl])
                for hh in range(2):
                    pd = P.bank()
                    for h4 in range(4):
                        h = hh * 4 + h4
                        o_ = pd[:nt, h4 * 128:h4 * 128 + nt]
                        P.mm(o_, dta_[:nt, h:h + 1].bc([nt, nt]), TI[:nt, :nt], start=True, stop=False)
                        P.mm(o_, identb[:nt, :nt], negm[:nt, :nt], start=False, stop=True)
                        dc = T("dec", dec)
                        P.act(dc[:nt, :nt], o_, AF.Exp, bias=nacol[:nt, h:h + 1])
                        g = h // 4
                        P.stt(mixT[h][:nt, :nt], pcb[:nt, g * 128:g * 128 + nt], dt_[:nt, h:h + 1], dc[:nt, :nt],
                              ALU.mult, ALU.mult)
                        yield
                pyd = P.bank(); pyo = P.bank()
                for h in range(8):
                    P.mm(pyd[:nt, h * 64:(h + 1) * 64], mixT[h][:nt, :nt], xB[j][:nt, h * 64:(h + 1) * 64])
                for g in range(2):
                    P.mm(pyo[:nt, g * 256:(g + 1) * 256], xbcT[6 + g][:, sl], Sb[:, g * 256:(g + 1) * 256])
                yield
                y1 = T("tf", tf); y2 = T("tf", tf)
                h3 = "p (h d) -> p h d"
                P.tt(y1[:nt].rr(h3, h=8), pyo[:nt].rr(h3, h=8), eacol[:nt].unsq(2).bc([nt, 8, 64]), ALU.mult)
                P.tt(y1[:nt], y1[:nt], pyd[:nt], ALU.add)
                P.tt(y2[:nt].rr(h3, h=8), xB[j][:nt, 0:512].rr(h3, h=8), dsk_bc[:nt].unsq(2).bc([nt, 8, 64]), ALU.mult)
                P.tt(y1[:nt], y1[:nt], y2[:nt], ALU.add)
                P.tt(y1[:nt], y1[:nt], sz[:nt], ALU.mult)
                yield
                nr = NRM()
                yn = T("tb", tb)
                P.act(yn[:nt], y1[:nt], AF.Square, accum=nr[:nt, 0:1])
                t8 = T("sm", sm)
                rstd_from_ss(nr[:nt, 1:2], nr[:nt, 0:1], 512, t8[:nt, 0:1])
                P.act(yn[:nt], y1[:nt], AF.Copy, scale=nr[:nt, 1:2])
                pbk = P.bank().bitcast(BF16)
                for fc in range(4):
                    P.tr(pbk[:, fc * 128:fc * 128 + nt], yn[:nt, fc * 128:(fc + 1) * 128], identb[:nt, :nt])
                P.copy(mxS[:, :, sl], pbk[:, 0:512].rr("p (a t) -> p a t", a=4)[:, :, 0:nt])
                yield
                xw = T("tb", tb)
                P.tt(xw[:nt].rr(h3, h=8), xB[j][:nt, 0:512].rr(h3, h=8), dw[:nt].unsq(2).bc([nt, 8, 64]), ALU.mult)
                pcs = P.bank()
                for g in range(2):
                    P.mm(pcs[:, g * 256:(g + 1) * 256], xB[j][:nt, 512 + g * 128:512 + (g + 1) * 128],
                         xw[:nt, g * 256:(g + 1) * 256])
                P.tt(Sf.rr(h3, h=8), Sf.rr(h3, h=8), cdec.unsq(2).bc([128, 8, 64]), ALU.mult)
                P.tt(Sf, Sf, pcs, ALU.add)
                P.copy(Sb, Sf, eng="act")
                yield
            if mt == nmac - 1:
                pbs = P.bank()
                for a4 in range(4):
                    P.tr(pbs[:, a4 * 128:(a4 + 1) * 128], Sf[:, a4 * 128:(a4 + 1) * 128], identf)
                so = T("tf", tf)
                P.copy(so, pbs)
                sdst = ssms_d[0] if is_sample else ssmp_d[s]
                P.dma(D_(sdst.rearrange("(a p) n -> p a n", p=128), "ssmout"), so.rr("p (a n) -> p a n", a=4), eng="pool")

            yield

        def attn(s, is_sample, mt, bg, bg_units):
            nsub, nt = (1, DEC_SEQ) if is_sample else (4, 128)
            NT = nsub * nt
            nmac = 1 if is_sample else seqlen // 512
            xsrc = xsd if is_sample else xpd
            row_base = 0 if is_sample else s * seqlen
            x1row_base = NTOK if is_sample else s * seqlen
            kout = ksd if is_sample else kpd
            vout = vsd if is_sample else vpd
            past_blocks = PAST // 128 if is_sample else 0
            tok0 = mt * 512
            qT = qT2[(s * (seqlen // 512) + mt) % 2]
            mxS = mxS2[(s * (seqlen // 512) + mt) % 2]
            if is_sample:
                blocks = [(PAST, DEC_SEQ, past_blocks, 0)] + [(b * 128, 128, b, None) for b in range(past_blocks - 1, -1, -1)]
            else:
                nb = mt * 4 + 4
                blocks = [(b * 128, 128, b, (b - mt * 4) if b >= mt * 4 else None) for b in range(nb - 1, -1, -1)]
            NQ = NT
            items = [(h, bi, blk) for h in range(8) for bi, blk in enumerate(blocks)]
            nblk = len(blocks)
            state = {}
            P.bank()
            P.rots["bg"] = [5, 6, 7]
            Rbank = [P.banks[0], P.banks[0]]
            Obank = [P.banks[1], P.banks[2]]

            pSd = {}

            def stageQ(it):
                h, bi, (col0, nk, vblk, jj) = it
                q0 = jj * nt if (jj is not None and not is_sample) else 0
                pr, pb0 = h // 2, (h % 2) * 64
                pS = P.bank("att")
                P.mm(pS[:nk, q0:NQ], kTb[vblk][pb0:pb0 + 64, pr, 0:nk], qT[pb0:pb0 + 64, pr, q0:NQ])
                pSd[(h, bi)] = pS

            def stageA1(it):
                h, bi, (col0, nk, vblk, jj) = it
                q0 = jj * nt if (jj is not None and not is_sample) else 0
                pS = pSd.pop((h, bi))
                sl_ = cnt["E"] % NBUF
                s2_ = cnt["E"] % 2
                cnt["E"] += 1
                E = Eb[sl_]; SP = SPb[sl_]; X = Xb[s2_]; W = Wb[s2_]
                state[(h, bi)] = (E, SP, X, W)
                P.act(E[:nk, q0:NQ], pS[:nk, q0:NQ], AF.Exp, scale=0.125)
                if jj is not None:
                    P.asel(E[:nk, q0:NQ], E[:nk, q0:NQ], [[1, NQ - q0]], ALU.is_ge, 0.0, q0 - jj * 128 - 1, -1)

            def stageA2(it):
                h, bi, (col0, nk, vblk, jj) = it
                q0 = jj * nt if (jj is not None and not is_sample) else 0
                E, SP, X, W = state[(h, bi)]
                P.act(SP[:nk, q0:NQ], E[:nk, q0:NQ], AF.Ln, bias=1.0)

            def stageB1a(it):
                h, bi, (col0, nk, vblk, jj) = it
                q0 = jj * nt if (jj is not None and not is_sample) else 0
                E, SP, X, W = state[(h, bi)]
                R = Rbank[h % 2]
                P.mm(R[:, q0:NQ], TLr[:nk, :], SP[:nk, q0:NQ], start=(bi == 0), stop=False, skip=True)

            def stageB1b(it):
                h, bi, (col0, nk, vblk, jj) = it
                q0 = jj * nt if (jj is not None and not is_sample) else 0
                E, SP, X, W = state[(h, bi)]
                R = Rbank[h % 2]
                P.act(X[:nk, q0:NQ], R[:nk, q0:NQ], AF.Exp, scale=-1.0)

            def stageB2(it):
                h, bi, (col0, nk, vblk, jj) = it
                q0 = jj * nt if (jj is not None and not is_sample) else 0
                E, SP, X, W = state[(h, bi)]
                R = Rbank[h % 2]
                if bi != nblk - 1:
                    P.mm(R[:, q0:NQ], OmTr[:nk, :], SP[:nk, q0:NQ], start=False, stop=False, skip=True)
                P.tt(W[:nk, q0:NQ], E[:nk, q0:NQ], X[:nk, q0:NQ], ALU.mult)

            def stageC(it):
                h, bi, (col0, nk, vblk, jj) = it
                q0 = jj * nt if (jj is not None and not is_sample) else 0
                E, SP, X, W = state.pop((h, bi))
                O = Obank[h % 2]
                i0 = q0 // nt
                for i in range(i0, nsub):
                    P.mm(O[:nt, i * 64:(i + 1) * 64], W[:nk, i * nt:(i + 1) * nt], vtb[vblk][:nk, h * 64:(h + 1) * 64],
                         start=(bi == 0 and i == i0), stop=False, skip=True)
                if NDUMMY and not is_sample:
                    for _ in range(NDUMMY):
                        P.mm(O[:, 256:512], zl, zr, start=False, stop=False, skip=True)
                if bi == nblk - 1:
                    P.copy(oacc[:nt, 0:nsub, h * 64:(h + 1) * 64], O[:nt, 0:nsub * 64].rr("p (a d) -> p a d", a=nsub))

            n_it = len(items)
            per_step = -(-bg_units // max(n_it - 4, 1)) if bg is not None else 0
            stageQ(items[0])
            for step in range(n_it + 2):
                if 0 <= step - 1 < n_it:
                    stageB1a(items[step - 1])
                if step + 1 < n_it:
                    stageQ(items[step + 1])
                if step < n_it:
                    stageA1(items[step])
                if 0 <= step - 1 < n_it:
                    stageB1b(items[step - 1])
                if step < n_it:
                    stageA2(items[step])
                if 0 <= step - 2 < n_it:
                    stageC(items[step - 2])
                if 0 <= step - 1 < n_it:
                    stageB2(items[step - 1])
                if bg is not None:
                    for _ in range(per_step):
                        if next(bg, "done") == "done":
                            bg = None
                            break
            if bg is not None:
                for _ in bg:
                    pass
            P.rots["bg"] = list(range(8))

        def tail(s, is_sample, mt):
            nsub, nt = (1, DEC_SEQ) if is_sample else (4, 128)
            NT = nsub * nt
            nmac = 1 if is_sample else seqlen // 512
            xsrc = xsd if is_sample else xpd
            row_base = 0 if is_sample else s * seqlen
            x1row_base = NTOK if is_sample else s * seqlen
            kout = ksd if is_sample else kpd
            vout = vsd if is_sample else vpd
            past_blocks = PAST // 128 if is_sample else 0
            tok0 = mt * 512
            qT = qT2[(s * (seqlen // 512) + mt) % 2]
            mxS = mxS2[(s * (seqlen // 512) + mt) % 2]
            for i in range(nsub):
                nr = NRM()
                on = T("tb", tb)
                P.act(on[:nt], oacc[:nt, i, :], AF.Square, accum=nr[:nt, 0:1])
                t8 = T("sm", sm)
                rstd_from_ss(nr[:nt, 1:2], nr[:nt, 0:1], 512, t8[:nt, 0:1])
                P.act(on[:nt], oacc[:nt, i, :], AF.Copy, scale=nr[:nt, 1:2])
                pbk = P.bank().bitcast(BF16)
                for fc in range(4):
                    P.tr(pbk[:, fc * 128:fc * 128 + nt], on[:nt, fc * 128:(fc + 1) * 128], identb[:nt, :nt])
                P.copy(mxA[:, :, i * nt:(i + 1) * nt], pbk[:, 0:512].rr("p (a t) -> p a t", a=4)[:, :, 0:nt])
            for j in range(nsub):
                r0 = row_base + tok0 + j * nt
                xr_ = xr[0]
                P.dma(xr_[:nt], xsrc[r0:r0 + nt, :])
                for hf in range(2):
                    po = P.bank()
                    for fc in range(8):
                        src = mxA if fc < 4 else mxS
                        P.mm(po[:nt, :], src[:, fc % 4, j * nt:(j + 1) * nt], wout[:, fc, hf * 512:(hf + 1) * 512],
                             start=(fc == 0), stop=(fc == 7))
                    t1 = T("tf", tf)
                    P.tt(t1[:nt], po[:nt, :], g1bc[:nt, hf * 512:(hf + 1) * 512], ALU.mult)
                    P.tt(xr_[:nt, hf * 512:(hf + 1) * 512], xr_[:nt, hf * 512:(hf + 1) * 512], t1[:nt], ALU.add)
                x1r = x1row_base + tok0 + j * nt
                P.dma(x1s[x1r:x1r + nt, :], xr_[:nt], eng="pool")


        def drain(g):
            n = 0
            for _ in g:
                n += 1
            return n

        def load_g1bc(s):
            P.dma(g1bc, V(modd_d[s:s + 1, 2 * D:3 * D].to_broadcast([128, D]), modd.res))

        def next_seq_bg(s1):
            seq_setup(s1, False)
            yield
            yield from prep(s1, False, 0, kv="skip")

        def run_all():
            nmac = seqlen // 512
            load_g1bc(0)
            seq_setup(0, False)
            units = drain(prep(0, False, 0))
            for s in range(nseq):
                for mt in range(nmac):
                    deferred = None
                    if mt + 1 < nmac:
                        bg = prep(s, False, mt + 1)
                    elif s + 1 < nseq:
                        bg = next_seq_bg(s + 1)
                        deferred = s + 1
                    else:
                        bg = None
                    attn(s, False, mt, bg, units)
                    tail(s, False, mt)
                    if deferred is not None:
                        load_g1bc(deferred)
                        drain(prep(deferred, False, 0, kv="only"))
            if do_sample:
                load_g1bc(nseq)
                seq_setup(nseq, True)
                drain(prep(nseq, True, 0))
                attn(nseq, True, 0, None, units)
                tail(nseq, True, 0)

        try:
            run_all()
        except StopBuild:
            P.emit()
            return nc
        P.barrier()

        P.pool("F", cend)
        wg = A("F", [128, 8, DFF], BF16, "wg"); wu = A("F", [128, 8, DFF], BF16, "wu")
        wd = A("F", [128, NFF, D], BF16, "wd")
        xf = [A("F", [128, D], F32, f"xf{i}") for i in range(4)]
        fst = [A("F", [128, DFF], F32, f"fst{i}") for i in range(2)]
        wgd = D_(wg_d, "wgd"); wud = D_(wu_d, "wud"); wdd = D_(wd_d, "wdd")
        li = 0
        for kc in range(8 if do_f else 0):
            for (wdst, wsrc) in ((wg, wgd), (wu, wud)):
                s_ = fst[li % 2]; li += 1
                P.dma(s_, wsrc[kc * 128:(kc + 1) * 128, :])
                if li % 2 == 0:
                    P.copy(wdst[:, kc, :], s_)
                else:
                    P.copy(wdst[:, kc, :], s_, eng="act")
        for f2 in range(NFF // 2 if do_f else 0):
            s_ = fst[li % 2]; li += 1
            P.dma(s_[:, 0:2 * D].rr("p (a n) -> p a n", a=2), wdd[f2 * 256:(f2 + 1) * 256, :].rr("(a p) n -> p a n", p=128))
            if li % 2 == 0:
                P.copy(wd[:, 2 * f2:2 * f2 + 2, :], s_[:, 0:2 * D].rr("p (a n) -> p a n", a=2))
            else:
                P.copy(wd[:, 2 * f2:2 * f2 + 2, :], s_[:, 0:2 * D].rr("p (a n) -> p a n", a=2), eng="act")
        P.barrier()
        P.pools["F"][1] -= 2 * DFF
        aT = A("F", [128, NFF, 512], BF16, "aT")
        h2T = A("F", [128, 8, 512], BF16, "h2T")
        fjunk = A("F", [128, D], BF16, "fjunk"); fxn = A("F", [128, D], BF16, "fxn")
        sg = [A("F", [128, 512], F32, f"sg{i}") for i in range(2)]
        yo = A("F", [128, D], F32, "yo")
        g2bc = A("F", [128, D], F32, "g2bc")
        fss = A("F", [128, 4], F32, "fss"); frs = A("F", [128, 4], F32, "frs"); ftm = A("F", [128, 4], F32, "ftm")
        print("SBUF words: F end", P.pools["F"][1], "of", sb_words)
        fc_ = {"sg": 0}

        def ffn_macro(s, r0, nsub, nt, ydst, yrow):
            NT = nsub * nt
            for j in range(nsub):
                P.dma(xf[j][:nt], x1s[r0 + j * nt:r0 + (j + 1) * nt, :])
                P.act(fjunk[:nt], xf[j][:nt], AF.Square, accum=fss[:nt, 0:1])
                rstd_from_ss(frs[:nt, 0:1], fss[:nt, 0:1], D, ftm[:nt, 0:1])
                P.act(fxn[:nt], xf[j][:nt], AF.Copy, scale=frs[:nt, 0:1])
                pbk = P.bank().bitcast(BF16)
                for kc in range(8):
                    P.tr(pbk[:, kc * 128:kc * 128 + nt], fxn[:nt, kc * 128:(kc + 1) * 128], identb[:nt, :nt])
                for kc in range(8):
                    P.act(h2T[:, kc, j * nt:(j + 1) * nt], pbk[:, kc * 128:kc * 128 + nt], AF.Identity,
                          scale=A2[:, s, kc:kc + 1], bias=S2[:, s, kc:kc + 1])
            for ffc in range(NFF):
                pg = P.bank(); pu = P.bank()
                for kc in range(8):
                    P.mm(pg[:, :NT], wg[:, kc, ffc * 128:(ffc + 1) * 128], h2T[:, kc, 0:NT], start=(kc == 0), stop=(kc == 7))
                for kc in range(8):
                    P.mm(pu[:, :NT], wu[:, kc, ffc * 128:(ffc + 1) * 128], h2T[:, kc, 0:NT], start=(kc == 0), stop=(kc == 7))
                sg_ = sg[fc_["sg"] % 2]; fc_["sg"] += 1
                P.act(sg_[:, :NT], pg[:, :NT], AF.Silu)
                P.tt(aT[:, ffc, 0:NT], sg_[:, :NT], pu[:, :NT], ALU.mult)
            for j in range(nsub):
                for hf in range(2):
                    py = P.bank()
                    for ffc in range(NFF):
                        P.mm(py[:nt, :], aT[:, ffc, j * nt:(j + 1) * nt], wd[:, ffc, hf * 512:(hf + 1) * 512],
                             start=(ffc == 0), stop=(ffc == NFF - 1))
                    hs = slice(hf * 512, (hf + 1) * 512)
                    P.tt(yo[:nt, hs], py[:nt, :], g2bc[:nt, hs], ALU.mult)
                    P.tt(yo[:nt, hs], yo[:nt, hs], xf[j][:nt, hs], ALU.add)
                P.dma(ydst[yrow + j * nt:yrow + (j + 1) * nt, :], yo[:nt], eng="pool")

        for s in range(NS if do_f else 0):
            P.dma(g2bc, V(modd_d[s:s + 1, 5 * D:6 * D].to_broadcast([128, D]), modd.res))
            if s < nseq:
                for mt in range(seqlen // 512):
                    r0 = s * seqlen + mt * 512
                    ffn_macro(s, r0, 4, 128, ypd, r0)
            else:
                ffn_macro(s, NTOK, 1, DEC_SEQ, ysd, 0)

        global LAST_PROG
        LAST_PROG = P
        P.emit()
        print("ops", P.n, "waits", P.nwaits, {e: len(P.ops[e]) for e in P.ENGS})
    return nc


_NC_CACHE = {}


def _get_nc(nseq, seqlen):
    key = (nseq, seqlen)
    if key not in _NC_CACHE:
        _NC_CACHE[key] = build(nseq, seqlen)
    return _NC_CACHE[key]


def make_in_maps(inputs, ncores, nseq, seqlen):
    f = lambda a: np.ascontiguousarray(np.asarray(a, dtype=np.float32))
    xp = f(inputs["x_prompt"]); xs = f(inputs["x_sample"])
    ck = f(inputs["cache_k"])[0]; cv = f(inputs["cache_v"])[0]
    sconv = f(inputs["state_conv"])[0]; sssm = f(inputs["state_ssm"])[0]
    cp = f(inputs["c_prompt"]); cs = f(inputs["c_sample"])
    wnames = ["w_ada", "b_ada", "g_norm1", "w_in", "g_q", "g_k", "g_attn_out", "conv_w", "conv_b", "dt_bias",
              "a_log", "d_skip", "g_ssm_out", "w_out", "g_norm2", "w_gate", "w_up", "w_down"]
    shared = {}
    for n in wnames:
        a = f(inputs[n])[0]
        if a.ndim == 1:
            a = a[None, :]
        shared[n] = np.ascontiguousarray(a)
    maps = []
    for c in range(ncores):
        m = dict(shared)
        m["xp"] = np.ascontiguousarray(xp[c * nseq:(c + 1) * nseq].reshape(nseq * seqlen, D))
        m["xs"] = np.ascontiguousarray(xs[c])
        m["ck"] = np.ascontiguousarray(ck[c].reshape(PAST, 512))
        m["cv"] = np.ascontiguousarray(cv[c].reshape(PAST, 512))
        m["sconv"] = np.ascontiguousarray(sconv[c])
        m["sssm"] = np.ascontiguousarray(sssm[c].reshape(512, 128))
        m["cvec"] = np.ascontiguousarray(np.concatenate([cp[c * nseq:(c + 1) * nseq], cs[c:c + 1]], axis=0))
        maps.append(m)
    return maps


def gather(results, ncores, nseq, seqlen):
    cat = lambda k: [np.asarray(r[k]) for r in results]
    yp = np.concatenate([a.reshape(nseq, seqlen, D) for a in cat("yp")], 0)
    ys = np.stack(cat("ys"), 0)
    kp = np.concatenate([a.reshape(nseq, seqlen, NH, HD) for a in cat("kp")], 0)[None]
    vp = np.concatenate([a.reshape(nseq, seqlen, NH, HD) for a in cat("vp")], 0)[None]
    convp = np.concatenate(cat("convp"), 0)[None]
    ssmp = np.concatenate([a.reshape(nseq, NH, HD, DST) for a in cat("ssmp")], 0)[None]
    ks = np.stack([a.reshape(DEC_SEQ, NH, HD) for a in cat("ks")], 0)[None]
    vs = np.stack([a.reshape(DEC_SEQ, NH, HD) for a in cat("vs")], 0)[None]
    convs = np.concatenate(cat("convs"), 0)[None]
    ssms = np.concatenate([a.reshape(1, NH, HD, DST) for a in cat("ssms")], 0)[None]
    return tuple(np.ascontiguousarray(a, dtype=np.float32) for a in (yp, ys, kp, vp, convp, ssmp, ks, vs, convs, ssms))


def kernel(**inputs):
    nseq, seqlen = 4, 2048
    nc = _get_nc(nseq, seqlen)
    maps = make_in_maps(inputs, NCORES, nseq, seqlen)
    res = run_bass_kernel_spmd(nc, maps, core_ids=list(range(NCORES)))
    return gather(res.results, NCORES, nseq, seqlen)
```
